# Optimizing a Trainium2 kernel written in Bass

```python
import jax, jax.numpy as jnp
from jax import lax
import numpy as np

D_MODEL = 1024
BATCH = 8
SEQ = 2048
DEPTH = 1

HEAD_DIM = 64
NA_HEADS = 8
NA_WIDTH = NA_HEADS * HEAD_DIM
NA_KH = 8
NA_KW = 16
GRID_W = 64
SWA_Q_HEADS = 8
SWA_KV_HEADS = 2
SWA_WIDTH = SWA_Q_HEADS * HEAD_DIM
SWA_KV_WIDTH = SWA_KV_HEADS * HEAD_DIM
SWA_WINDOW = 128
SWA_BLOCK = 128
ROPE_THETA = 10000.0

MIX_WIDTH = NA_WIDTH + SWA_WIDTH
LN_EPS = 1e-5
MASK_VALUE = -1e30
SECTION_WIDTHS = (NA_WIDTH, NA_WIDTH, NA_WIDTH, NA_WIDTH, SWA_WIDTH, SWA_KV_WIDTH, SWA_KV_WIDTH, SWA_WIDTH)
IN_WIDTH = sum(SECTION_WIDTHS)
SPLIT_POINTS = tuple(sum(SECTION_WIDTHS[:i + 1]) for i in range(len(SECTION_WIDTHS) - 1))
DEEPNORM_ALPHA = (2.0 * DEPTH) ** 0.25
DEEPNORM_BETA = (8.0 * DEPTH) ** -0.25

kernel_name = "hymba_natten_swa_deepnorm_encoder"


def layer_norm(x, gain, bias):
    xf = x.astype(jnp.float32)
    mu = jnp.mean(xf, axis=-1, keepdims=True)
    var = jnp.mean(jnp.square(xf - mu), axis=-1, keepdims=True)
    y = (xf - mu) * lax.rsqrt(var + LN_EPS) * gain.astype(jnp.float32) + bias.astype(jnp.float32)
    return y.astype(x.dtype)


def rotary(t, positions):
    d = t.shape[-1]
    inv_freq = ROPE_THETA ** (-jnp.arange(0, d, 2, dtype=jnp.float32) / d)
    ang = positions.astype(jnp.float32)[:, None] * inv_freq[None, :]
    cos = jnp.cos(ang)[None, :, None, :]
    sin = jnp.sin(ang)[None, :, None, :]
    tf = t.astype(jnp.float32)
    t1, t2 = tf[..., : d // 2], tf[..., d // 2:]
    out = jnp.concatenate([t1 * cos - t2 * sin, t2 * cos + t1 * sin], axis=-1)
    return out.astype(t.dtype)


def neighborhood_attention(q, k, v, rel_bias):
    B, S, H, D = q.shape
    rows = S // GRID_W
    kh = min(NA_KH, rows)
    kw = NA_KW
    qg = q.reshape(B, rows, GRID_W, H, D)
    kg = k.reshape(B, rows, GRID_W, H, D)
    vg = v.reshape(B, rows, GRID_W, H, D)
    cols = jnp.arange(GRID_W)
    col_start = jnp.clip(cols - kw // 2, 0, GRID_W - kw)
    col_idx = col_start[:, None] + jnp.arange(kw)[None, :]
    dc_idx = col_idx - cols[:, None] + (NA_KW - 1)
    scale = D ** -0.5

    def row_block(r):
        r0 = jnp.clip(r - kh // 2, 0, rows - kh)
        q_r = lax.dynamic_index_in_dim(qg, r, axis=1, keepdims=False)
        k_slab = lax.dynamic_slice_in_dim(kg, r0, kh, axis=1)
        v_slab = lax.dynamic_slice_in_dim(vg, r0, kh, axis=1)
        k_win = k_slab[:, :, col_idx]
        v_win = v_slab[:, :, col_idx]
        s = jnp.einsum('bqhd,biqjhd->bhqij', q_r, k_win).astype(jnp.float32) * scale
        dr_idx = r0 + jnp.arange(kh) - r + (NA_KH - 1)
        bias = rel_bias[:, dr_idx[None, :, None], dc_idx[:, None, :]]
        s = s + bias.astype(jnp.float32)[None]
        p = jax.nn.softmax(s.reshape(B, H, GRID_W, kh * kw), axis=-1).reshape(B, H, GRID_W, kh, kw)
        return jnp.einsum('bhqij,biqjhd->bqhd', p.astype(v.dtype), v_win)

    out = lax.map(row_block, jnp.arange(rows))
    return jnp.transpose(out, (1, 0, 2, 3, 4)).reshape(B, S, H * D)


def windowed_gqa_with_sink(q, k, v, sink):
    B, S, HQ, D = q.shape
    HKV = k.shape[2]
    G = HQ // HKV
    wb = SWA_BLOCK
    nb = S // wb
    qb = q.reshape(B, nb, wb, HKV, G, D)
    pad = ((0, 0), (wb, wb), (0, 0), (0, 0))
    kp = jnp.pad(k, pad).reshape(B, nb + 2, wb, HKV, D)
    vp = jnp.pad(v, pad).reshape(B, nb + 2, wb, HKV, D)
    k_band = jnp.concatenate([kp[:, :-2], kp[:, 1:-1], kp[:, 2:]], axis=2)
    v_band = jnp.concatenate([vp[:, :-2], vp[:, 1:-1], vp[:, 2:]], axis=2)
    s = jnp.einsum('bnqkgd,bnjkd->bnkgqj', qb, k_band).astype(jnp.float32) * (D ** -0.5)
    q_off = jnp.arange(wb)[:, None]
    k_off = jnp.arange(3 * wb)[None, :] - wb
    k_abs = jnp.arange(nb)[:, None, None] * wb + k_off[None]
    valid = (jnp.abs(k_off - q_off) <= SWA_WINDOW)[None] & (k_abs >= 0) & (k_abs < S)
    s = jnp.where(valid[None, :, None, None], s, MASK_VALUE)
    sink_col = jnp.broadcast_to(sink.astype(jnp.float32).reshape(1, 1, HKV, G, 1, 1), s.shape[:-1] + (1,))
    p = jax.nn.softmax(jnp.concatenate([s, sink_col], axis=-1), axis=-1)[..., :-1]
    out = jnp.einsum('bnkgqj,bnjkd->bnqkgd', p.astype(v.dtype), v_band)
    return out.reshape(B, S, HQ * D)


def setup_inputs(seed: int = 0) -> dict:
    key = jax.random.key(seed)
    k_x, k_in, k_rpb, k_sink, k_out, k_g, k_b = jax.random.split(key, 7)
    x = jax.random.normal(k_x, (BATCH, SEQ, D_MODEL), jnp.float32)
    col_scale = jnp.concatenate([
        jnp.ones((2 * NA_WIDTH,), jnp.float32),
        jnp.full((NA_WIDTH,), DEEPNORM_BETA, jnp.float32),
        jnp.ones((NA_WIDTH + SWA_WIDTH + SWA_KV_WIDTH,), jnp.float32),
        jnp.full((SWA_KV_WIDTH,), DEEPNORM_BETA, jnp.float32),
        jnp.ones((SWA_WIDTH,), jnp.float32),
    ])
    w_in = jax.random.normal(k_in, (DEPTH, D_MODEL, IN_WIDTH), jnp.float32) * (D_MODEL ** -0.5) * col_scale
    rel_pos_bias = 0.02 * jax.random.normal(k_rpb, (DEPTH, NA_HEADS, 2 * NA_KH - 1, 2 * NA_KW - 1), jnp.float32)
    sink_logits = 0.5 * jax.random.normal(k_sink, (DEPTH, SWA_Q_HEADS), jnp.float32)
    w_out = jax.random.normal(k_out, (DEPTH, MIX_WIDTH, D_MODEL), jnp.float32) * (MIX_WIDTH ** -0.5) * DEEPNORM_BETA
    ln_gain = 1.0 + 0.02 * jax.random.normal(k_g, (DEPTH, D_MODEL), jnp.float32)
    ln_bias = 0.02 * jax.random.normal(k_b, (DEPTH, D_MODEL), jnp.float32)
    return {"x": x, "w_in": w_in, "rel_pos_bias": rel_pos_bias, "sink_logits": sink_logits,
            "w_out": w_out, "ln_gain": ln_gain, "ln_bias": ln_bias}


def reference(x, w_in, rel_pos_bias, sink_logits, w_out, ln_gain, ln_bias):
    B, S, _ = x.shape
    positions = jnp.arange(S, dtype=jnp.int32)
    for layer in range(DEPTH):
        h = jnp.einsum('bsd,de->bse', x, w_in[layer])
        q_a, k_a, v_a, z_a, q_b, k_b, v_b, z_b = jnp.split(h, SPLIT_POINTS, axis=-1)
        y_a = neighborhood_attention(q_a.reshape(B, S, NA_HEADS, HEAD_DIM),
                                     k_a.reshape(B, S, NA_HEADS, HEAD_DIM),
                                     v_a.reshape(B, S, NA_HEADS, HEAD_DIM),
                                     rel_pos_bias[layer])
        qb = rotary(q_b.reshape(B, S, SWA_Q_HEADS, HEAD_DIM), positions)
        kb = rotary(k_b.reshape(B, S, SWA_KV_HEADS, HEAD_DIM), positions)
        y_b = windowed_gqa_with_sink(qb, kb, v_b.reshape(B, S, SWA_KV_HEADS, HEAD_DIM), sink_logits[layer])
        mixed = jnp.concatenate([y_a * jax.nn.silu(z_a), y_b * jax.nn.silu(z_b)], axis=-1)
        y = jnp.einsum('bse,ed->bsd', mixed, w_out[layer])
        x = layer_norm(DEEPNORM_ALPHA * x + y, ln_gain[layer], ln_bias[layer])
    return x
```

```python
import numpy as np
from contextlib import ExitStack
import concourse.bass as bass
import concourse.mybir as mybir
from concourse.bass_utils import run_bass_kernel_spmd

F32 = mybir.dt.float32
BF = mybir.dt.bfloat16
AF = mybir.ActivationFunctionType
ALU = mybir.AluOpType

S = 2048
D = 1024
NEG = -30000.0
ALPHA = float(2.0 ** 0.25)
LN_EPS = 1e-5
N_CORES = 8
import os
ROT_ADD_ENG = os.environ.get('ROT_ADD_ENG', 'pool')
SAME_ENG_SYNC = int(os.environ.get('SAME_ENG_SYNC', '1'))
GROUP_S = int(os.environ.get('GROUP_S', '1'))
GROUP_PV = int(os.environ.get('GROUP_PV', '0'))
LOOKAHEAD = int(os.environ.get('LOOKAHEAD', '3'))
BIAS_PE = int(os.environ.get('BIAS_PE', '1'))
PREFETCH = int(os.environ.get('PREFETCH', '0'))
NA_RECIP_DVE = int(os.environ.get('NA_RECIP_DVE', '1'))
ROT_DBG = int(os.environ.get('ROT_DBG', '5'))

FM_TYPES = [("KA", k) for k in range(4)] + [("QA", k) for k in range(4)] + [("QB", k) for k in range(4)] + \
           [("KB", 0)] + [("ZA", k) for k in range(4)] + [("ZB", k) for k in range(4)]
N_FM = len(FM_TYPES)
FM_COLS = N_FM * 128
V_COLS = 640


def _w_in_perm():
    cols = []
    qa, ka, va, za, qb, kb, vb, zb = 0, 512, 1024, 1536, 2048, 2560, 2688, 2816
    for t, k in FM_TYPES:
        if t == "KA":
            cols += list(range(ka + 128 * k, ka + 128 * (k + 1)))
        elif t == "QA":
            cols += list(range(qa + 128 * k, qa + 128 * (k + 1)))
        elif t == "ZA":
            cols += list(range(za + 128 * k, za + 128 * (k + 1)))
        elif t == "QB":
            cols += list(range(qb + 64 * k, qb + 64 * (k + 1))) + list(range(qb + 64 * (4 + k), qb + 64 * (5 + k)))
        elif t == "ZB":
            cols += list(range(zb + 64 * k, zb + 64 * (k + 1))) + list(range(zb + 64 * (4 + k), zb + 64 * (5 + k)))
        elif t == "KB":
            cols += list(range(kb, kb + 128))
    cols += list(range(va, va + 512)) + list(range(vb, vb + 128))
    return np.array(cols, dtype=np.int64)


def _w_out_perm():
    rows = list(range(512))
    for j in range(4):
        rows += list(range(512 + 64 * j, 512 + 64 * (j + 1))) + list(range(512 + 64 * (4 + j), 512 + 64 * (5 + j)))
    return np.array(rows, dtype=np.int64)


def _na_table(rpb):
    jr = np.arange(128) // 64
    jc = np.arange(128) % 64
    qc = np.arange(64)
    cs = np.clip(qc - 8, 0, 48)
    colvalid = (jc[:, None] >= cs[None, :]) & (jc[:, None] < cs[None, :] + 16)
    dc_idx = np.clip(jc[:, None] - qc[None, :] + 15, 0, 30)
    tbl = np.full((128, 8, 14, 64), NEG, dtype=np.float32)
    for dri in range(14):
        dr0 = 6 - dri
        dr_idx = dr0 + jr + 7
        g = rpb[:, dr_idx[:, None], dc_idx]
        g = np.transpose(g, (1, 0, 2))
        tbl[:, :, dri, :] = np.where(colvalid[:, None, :], g, np.float32(NEG))
    return np.ascontiguousarray(tbl.reshape(128, 8 * 14 * 64))


def _consts():
    ident = np.eye(128, dtype=np.float32)
    swap = np.zeros((128, 128), np.float32)
    for m in range(128):
        swap[(m + 64) % 128, m] = 1.0
    perm = np.zeros((128, 128), np.float32)
    for p in range(128):
        if p % 64 < 32:
            perm[p + 32, p] = -1.0
        else:
            perm[p - 32, p] = 1.0
    j = np.arange(128)[:, None]
    q = np.arange(128)[None, :]
    mprev = np.where(j >= q, 0.0, NEG).astype(np.float32)
    mnext = np.where(j <= q, 0.0, NEG).astype(np.float32)
    masks = np.concatenate([np.tile(mprev, (1, 4)), np.tile(mnext, (1, 4))], axis=1)
    cmat = np.concatenate([ident, swap, perm, masks], axis=1)
    inv_freq = (np.float32(10000.0) ** (-(np.arange(0, 64, 2, dtype=np.float32)) / np.float32(64))).astype(np.float32)
    ang = (np.arange(S, dtype=np.float32)[:, None] * inv_freq[None, :]).astype(np.float32)
    cosT = np.cos(ang).astype(np.float32).T
    sinT = np.sin(ang).astype(np.float32).T
    cosT = np.ascontiguousarray(np.tile(cosT, (4, 1)))
    sinT = np.ascontiguousarray(np.tile(sinT, (4, 1)))
    return np.ascontiguousarray(cmat), cosT, sinT


ENGS = ("pe", "act", "dve", "pool", "sp")


class Sched:
    def __init__(self):
        self.streams = {e: [] for e in ENGS}
        self.cnt = {e: 0 for e in ENGS}
        self.dma_cnt = {}
        self.lastw = {}
        self.readers = {}
        self.waited = {e: {} for e in ENGS}

    def _deps(self, reads, writes):
        deps = {}

        def add(ev):
            if ev is None:
                return
            sk, v = ev
            if deps.get(sk, 0) < v:
                deps[sk] = v
        for r in reads:
            add(self.lastw.get(r))
        for w in writes:
            add(self.lastw.get(w))
            for sk, v in self.readers.get(w, {}).items():
                add((sk, v))
        return deps

    def op(self, eng, fn, reads=(), writes=(), dma=None):
        writes = list(writes) + [r for r in reads if isinstance(r, tuple) and r[0] == "ps"]
        if eng == "pool" and dma is not None:
            writes.append("poolq")
        reads = [r for r in reads if not (isinstance(r, tuple) and r[0] == "ps")]
        deps = self._deps(reads, writes)
        waits = []
        for sk, v in deps.items():
            if sk == eng and (eng == "pe" or not SAME_ENG_SYNC):
                continue
            if self.waited[eng].get(sk, 0) >= v:
                continue
            self.waited[eng][sk] = v
            waits.append((sk, v))
        if dma is None:
            self.cnt[eng] += 1
            ev = (eng, self.cnt[eng])
            inc = (eng, 1)
        else:
            sk = ("dma", dma)
            self.dma_cnt[sk] = self.dma_cnt.get(sk, 0) + 16
            ev = (sk, self.dma_cnt[sk])
            inc = (sk, 16)
        self.streams[eng].append((waits, fn, inc))
        for r in reads:
            d = self.readers.setdefault(r, {})
            if d.get(ev[0], 0) < ev[1]:
                d[ev[0]] = ev[1]
        for w in writes:
            self.lastw[w] = ev
            self.readers[w] = {}
        return ev

    def barrier(self):
        cur = {e: self.cnt[e] for e in ENGS if self.cnt[e] > 0}
        cur.update(self.dma_cnt)
        for e in ENGS:
            waits = []
            for sk, v in cur.items():
                if sk == e:
                    continue
                if self.waited[e].get(sk, 0) >= v:
                    continue
                self.waited[e][sk] = v
                waits.append((sk, v))
            if waits:
                self.streams[e].append((waits, None, None))


class Ring:
    def __init__(self, items):
        self.items = list(items)
        self.i = 0

    def next(self):
        v = self.items[self.i % len(self.items)]
        self.i += 1
        return v

    def next_pair(self):
        if self.i % 2 == 1:
            self.i += 1
        a = self.items[self.i % len(self.items)]
        self.i += 2
        return a


def _runs(Rs):
    out = []
    i = 0
    while i < len(Rs):
        if i + 1 < len(Rs):
            st = Rs[i + 1] - Rs[i]
            j = i + 1
            while j + 1 < len(Rs) and Rs[j + 1] - Rs[j] == st:
                j += 1
            out.append((Rs[i], st, j - i + 1))
            i = j + 1
        else:
            out.append((Rs[i], 1, 1))
            i += 1
    return out


def _pruns(Rs):
    out = []
    for par in (0, 1):
        sub = [R for R in Rs if R % 2 == par]
        i = 0
        while i < len(sub):
            j = i
            while j + 1 < len(sub) and sub[j + 1] - sub[j] == 2:
                j += 1
            out.append((sub[i], 2, j - i + 1))
            i = j + 1
    return out


def _na_jobs(QB):
    by = {}
    for R in range(8 * QB, 8 * QB + 8):
        r0 = min(max(R - 4, 0), 24)
        for a0 in (r0, r0 + 2, r0 + 4, r0 + 6):
            by.setdefault(a0, []).append(R)
    return sorted(by.items())


def build_nc(stop=None):
    nc = bass.Bass("TRN2", target_bir_lowering=False)
    x_d = nc.dram_tensor("x", [S, D], F32, kind="ExternalInput").ap()
    win_d = nc.dram_tensor("w_in_p", [D, FM_COLS + V_COLS], F32, kind="ExternalInput").ap()
    wout_d = nc.dram_tensor("w_out_p", [D, D], F32, kind="ExternalInput").ap()
    tbl_d = nc.dram_tensor("tbl_a", [128, 8 * 14 * 64], F32, kind="ExternalInput").ap()
    cmat_d = nc.dram_tensor("cmat", [128, 1408], F32, kind="ExternalInput").ap()
    cos_d = nc.dram_tensor("cos_t", [128, S], F32, kind="ExternalInput").ap()
    sin_d = nc.dram_tensor("sin_t", [128, S], F32, kind="ExternalInput").ap()
    sink_d = nc.dram_tensor("sink_lay", [128, 4], F32, kind="ExternalInput").ap()
    gb_d = nc.dram_tensor("gain_bias", [128, 2 * D], F32, kind="ExternalInput").ap()
    out_d = nc.dram_tensor("out", [S, D], F32, kind="ExternalOutput").ap()
    dbg_d = None
    if stop is not None:
        dbg_d = nc.dram_tensor("dbg", [128, 53248], BF, kind="ExternalOutput").ap()

    ARENA_BYTES = 211456
    arena = nc.alloc_sbuf_tensor("arena", [128, ARENA_BYTES // 2], BF)
    ps = nc.alloc_psum_tensor("ps", [128, 4096], F32)

    class Carver:
        def __init__(self, base=0):
            self.off = base
            self.hi = base

        def get(self, fshape, dt):
            n = int(np.prod(fshape))
            nb = n * (4 if dt == F32 else 2)
            nb_al = (nb + 63) // 64 * 64
            off = self.off
            self.off += nb_al
            self.hi = max(self.hi, self.off)
            assert self.off <= ARENA_BYTES, f"SBUF arena overflow {self.off}"
            a = arena[:, off // 2:(off + nb) // 2]
            if dt == F32:
                a = a.bitcast(F32)
            if len(fshape) == 2:
                a = a.rearrange("p (a b) -> p a b", b=fshape[1])
            elif len(fshape) == 3:
                a = a.rearrange("p (a b c) -> p a b c", b=fshape[1], c=fshape[2])
            return a

    cv = Carver()
    QA_T = cv.get([4, S], BF)
    KA_T = cv.get([4, S], BF)
    gateA = cv.get([4, S], BF)
    gateB = cv.get([4, S], BF)
    QB_T = cv.get([4, S], BF)
    KB_T = cv.get([S], BF)
    VA = cv.get([16, 512], BF)
    VB = cv.get([16, 128], BF)
    cbf = cv.get([1408], BF)
    ident_f = cv.get([128], F32)
    ones_bf = cv.get([128], BF)
    sinkE = cv.get([4], F32)
    sink_raw = cv.get([4], F32)
    xf = [cv.get([D], F32) for _ in range(4)]
    ident_bf = cbf[:, 0:128]
    swap_bf = cbf[:, 128:256]
    perm_bf = cbf[:, 256:384]
    masks_bf = cbf[:, 384:1408].rearrange("p (k n) -> p k n", k=2)
    phase_base = cv.off
    ca = Carver(phase_base)
    xT = ca.get([8, S], BF)
    wb = [ca.get([8, 256], BF) for _ in range(3)]
    off_wv = ca.off
    wv = ca.get([8, V_COLS], BF)
    cosT = ca.get([S], F32)
    sinT = ca.get([S], F32)
    qraw = [ca.get([512], BF) for _ in range(2)]
    t2b = [ca.get([512], F32) for _ in range(2)]
    t3b = [ca.get([512], F32) for _ in range(2)]
    cb = Carver(phase_base)
    mixT = [cb.get([8, 512], BF) for _ in range(2)]
    PT = [cb.get([512], BF) for _ in range(4)]
    rdenb = [cb.get([512], F32) for _ in range(2)]
    g2b = [cb.get([512], F32) for _ in range(2)]
    stt = [cb.get([12], F32) for _ in range(4)]
    mvt = [cb.get([2], F32) for _ in range(4)]
    rst = [cb.get([2], F32) for _ in range(4)]
    VAo = cb.get([15, 512], BF)
    assert cb.off <= off_wv, (cb.off, off_wv)
    cb.off = off_wv
    Wout = cb.get([8, D], BF)
    tblA = cb.get([8, 14, 64], BF)
    gbt = cb.get([2 * D], F32)

    def bank(b, w=512):
        return ps[:, b * 512:b * 512 + w]

    sc = Sched()

    class _Stop(Exception):
        pass

    def dump(ap_src, ncols):
        sc.barrier()
        sc.op("sp", lambda e: e.dma_start(out=dbg_d[:, 0:ncols], in_=ap_src), dma=("xo", 0))
        raise _Stop()

    def body():

        sc.op("sp", lambda e: e.dma_start(out=ident_f, in_=cmat_d[:, 0:128]), writes=["ident_f"], dma="identf")
        sc.op("sp", lambda e: e.dma_start(out=sink_raw, in_=sink_d), writes=["sink_raw"], dma="sink")
        sc.op("pool", lambda e: e.memset(ones_bf, 1.0), writes=["ones"])
        sc.op("act", lambda e: e.activation(out=sinkE, in_=sink_raw, func=AF.Exp), reads=["sink_raw"], writes=["sinkE"])

        if stop == "setup":
            dump(cbf, 1408)
        def load_x(i):
            sl = i % 4
            sc.op("sp", lambda e, i=i, sl=sl: e.dma_start(out=xf[sl], in_=x_d[i * 128:(i + 1) * 128, :]),
                  writes=[("xf", sl)], dma=("xf", sl))
        for i in range(4):
            load_x(i)

        wblocks = []
        c0 = 0
        while c0 < N_FM:
            n = min(2, N_FM - c0)
            wblocks.append((c0, n))
            c0 += n
        wslot_of_chunk = {}

        def load_wblock(bi, after=()):
            c0, n = wblocks[bi]
            sl = bi % 3
            src = win_d[:, c0 * 128:(c0 + n) * 128].rearrange("(kc p) n -> p kc n", p=128)
            sc.op("pool", lambda e, sl=sl, n=n, src=src: e.dma_start(out=wb[sl][:, :, 0:n * 128], in_=src),
                  reads=list(after), writes=[("wb", sl)], dma=("wb", sl))
            for k in range(n):
                wslot_of_chunk[c0 + k] = (sl, k)

        sc.op("pool", lambda e: e.dma_start(out=wv, in_=win_d[:, FM_COLS:FM_COLS + V_COLS].rearrange("(kc p) n -> p kc n", p=128)),
              writes=["wv"], dma="wv")
        sc.op("pool", lambda e: e.dma_start(out=cbf, in_=cmat_d), writes=["cbf"], dma="cbf")
        if stop == "A0":
            dump(wv.rearrange("p a b -> p (a b)"), 8 * V_COLS)

        fm_ring = Ring([0, 1, 2, 3])
        perm_ring = Ring([4, 5])
        rot_i = [0]
        pending = []
        v_next = [0]

        def emit_v_job():
            i = v_next[0]
            if i >= 16:
                return
            v_next[0] += 1
            for kc in range(8):
                sc.op("pe", lambda e, kc=kc, i=i: e.matmul(bank(6), lhsT=xT[:, kc, i * 128:(i + 1) * 128], rhs=wv[:, kc, 0:512],
                                                            start=(kc == 0), stop=(kc == 7)),
                      reads=[("xT", i), "wv"], writes=[("ps", 6)])
            for kc in range(8):
                sc.op("pe", lambda e, kc=kc, i=i: e.matmul(bank(7, 128), lhsT=xT[:, kc, i * 128:(i + 1) * 128], rhs=wv[:, kc, 512:640],
                                                            start=(kc == 0), stop=(kc == 7)),
                      reads=[("xT", i), "wv"], writes=[("ps", 7)])
            sc.op("dve", lambda e, i=i: e.tensor_copy(out=VA[:, i, :], in_=bank(6)), reads=[("ps", 6)], writes=[("VA", i)])
            sc.op("act", lambda e, i=i: e.copy(out=VB[:, i, :], in_=bank(7, 128)), reads=[("ps", 7)], writes=[("VB", i)])

        for i in range(16):
            sl = i % 4
            b = 2 * (i % 2)
            for kc in range(8):
                sc.op("pe", lambda e, b=b, kc=kc, sl=sl: e.transpose(out=ps[:, b * 512 + kc * 128:b * 512 + (kc + 1) * 128],
                                                                      in_=xf[sl][:, kc * 128:(kc + 1) * 128], identity=ident_f),
                      reads=[("xf", sl), "ident_f"], writes=[("ps", b), ("ps", b + 1)])
            src = ps[:, b * 512:b * 512 + 1024].rearrange("p (k t) -> p k t", k=8)
            dst = xT[:, :, i * 128:(i + 1) * 128]
            if i % 2 == 0:
                sc.op("dve", lambda e, src=src, dst=dst: e.tensor_copy(out=dst, in_=src),
                      reads=[("ps", b), ("ps", b + 1)], writes=[("xT", i)])
            else:
                sc.op("act", lambda e, src=src, dst=dst: e.copy(out=dst, in_=src),
                      reads=[("ps", b), ("ps", b + 1)], writes=[("xT", i)])
            if i + 4 < 16:
                load_x(i + 4)
            if i == 11:
                sc.op("sp", lambda e: e.dma_start(out=cosT, in_=cos_d), writes=["cos"], dma="cos")
                sc.op("sp", lambda e: e.dma_start(out=sinT, in_=sin_d), writes=["sin"], dma="sin")
            if i == 11:
                load_wblock(0, after=[("xT", 11)])
                load_wblock(1, after=[("xT", 11)])
            if i >= 1:
                emit_v_job()
        if stop == "xT":
            dump(xT.rearrange("p a b -> p (a b)"), 16384)

        for c in range(N_FM):
            typ, k = FM_TYPES[c]
            bi = c // 2
            if c % 2 == 0 and bi + 2 < len(wblocks):
                pass
            sl, kk = wslot_of_chunk[c]
            for tb in range(4):
                b = fm_ring.next()
                tsl = slice(tb * 512, (tb + 1) * 512)
                for kc in range(8):
                    sc.op("pe", lambda e, b=b, kc=kc, sl=sl, kk=kk, tsl=tsl: e.matmul(
                        bank(b), lhsT=wb[sl][:, kc, kk * 128:(kk + 1) * 128], rhs=xT[:, kc, tsl], start=(kc == 0), stop=(kc == 7)),
                        reads=[("xT", 4 * tb), ("xT", 4 * tb + 1), ("xT", 4 * tb + 2), ("xT", 4 * tb + 3), ("wb", sl)], writes=[("ps", b)])
                while pending:
                    pending.pop(0)()
                if typ == "KA":
                    sc.op("dve", lambda e, b=b, k=k, tsl=tsl: e.tensor_copy(out=KA_T[:, k, tsl], in_=bank(b)),
                          reads=[("ps", b)], writes=[("KA", k, tb)])
                elif typ == "QA":
                    sc.op("act", lambda e, b=b, k=k, tsl=tsl: e.mul(out=QA_T[:, k, tsl], in_=bank(b), mul=0.125),
                          reads=[("ps", b)], writes=[("QA", k, tb)])
                elif typ in ("ZA", "ZB"):
                    dst = (gateA if typ == "ZA" else gateB)[:, k, tsl]
                    sc.op("act", lambda e, b=b, dst=dst: e.activation(out=dst, in_=bank(b), func=AF.Silu),
                          reads=[("ps", b)], writes=[(typ, k, tb)])
                else:
                    r = rot_i[0] % 2
                    rot_i[0] += 1
                    dst = QB_T[:, k, tsl] if typ == "QB" else KB_T[:, tsl]
                    sc.op("act", lambda e, b=b, r=r: e.copy(out=qraw[r], in_=bank(b)), reads=[("ps", b)], writes=[("qraw", r)])
                    if ROT_DBG >= 2:
                        sc.op("dve", lambda e, b=b, r=r, tsl=tsl: e.tensor_tensor(out=t2b[r], in0=bank(b), in1=cosT[:, tsl], op=ALU.mult),
                              reads=[("ps", b), "cos"], writes=[("t2", r)])

                    def rot_tail(b=b, r=r, tsl=tsl, dst=dst, typ=typ, k=k, tb=tb):
                        pb = perm_ring.next()
                        if ROT_DBG < 3:
                            return
                        sc.op("pe", lambda e: e.matmul(bank(pb), lhsT=perm_bf, rhs=qraw[r], start=True, stop=True),
                              reads=[("qraw", r), "cbf"], writes=[("ps", pb)])
                        if ROT_DBG < 4:
                            return
                        sc.op("dve", lambda e: e.tensor_tensor(out=t3b[r], in0=bank(pb), in1=sinT[:, tsl], op=ALU.mult),
                              reads=[("ps", pb), "sin"], writes=[("t3", r)])
                        if ROT_DBG < 5:
                            return
                        sc.op(ROT_ADD_ENG, lambda e: e.tensor_tensor(out=dst, in0=t2b[r], in1=t3b[r], op=ALU.add),
                              reads=[("t2", r), ("t3", r)], writes=[(typ, k, tb)])
                    pending.append(rot_tail)
                emit_v_job()
            if c == 4:
                while pending:
                    pending.pop(0)()
                while v_next[0] < 16:
                    emit_v_job()
            dead = ["wv", "cos", "sin"] + [(nm, r) for nm in ("qraw", "t2", "t3") for r in range(2)]
            if PREFETCH and c == 6:
                sc.op("sp", lambda e: e.dma_start(out=gbt, in_=gb_d), writes=["gbt"] + dead, dma="gbt")
            if PREFETCH and c == 9:
                sc.op("pool", lambda e: e.dma_start(out=tblA.rearrange("p a b c -> p (a b c)"), in_=tbl_d),
                      writes=["tblA"] + dead, dma="tblA")
            if PREFETCH and c == 14:
                sc.op("pool", lambda e: e.dma_start(out=Wout, in_=wout_d.rearrange("(kc p) n -> p kc n", p=128)),
                      writes=["Wout"] + dead, dma="Wout")
            if stop is not None and stop.startswith("AC") and c == int(stop[2:]):
                while pending:
                    pending.pop(0)()
                dump(arena[:, 0:53248], 53248)
            if c % 2 == 1 or c == N_FM - 1:
                nb_ = c // 2 + 2
                if nb_ < len(wblocks):
                    load_wblock(nb_)
        while pending:
            pending.pop(0)()
        while v_next[0] < 16:
            emit_v_job()

        sc.barrier()
        if stop == "A":
            dump(arena[:, 0:53248], 53248)

        if not PREFETCH:
            sc.op("pool", lambda e: e.dma_start(out=tblA.rearrange("p a b c -> p (a b c)"), in_=tbl_d), writes=["tblA"], dma="tblA")
            sc.op("pool", lambda e: e.dma_start(out=Wout, in_=wout_d.rearrange("(kc p) n -> p kc n", p=128)), writes=["Wout"], dma="Wout")
            sc.op("sp", lambda e: e.dma_start(out=gbt, in_=gb_d), writes=["gbt"], dma="gbt")
        sc.op("sp", lambda e: e.dma_start(out=VAo[0:64, :, :], in_=VA[64:128, 0:15, :]), writes=["VAo"], dma="vao0")
        sc.op("sp", lambda e: e.dma_start(out=VAo[64:128, :, :], in_=VA[0:64, 1:16, :]), writes=["VAo"], dma="vao1")

        bias_rr = [0]
        if not BIAS_PE:
            tflat = tblA.rearrange("p a b c -> p (a b c)")
            for k in range(14):
                sc.op("act", lambda e, k=k: e.activation(out=tflat[:, k * 512:(k + 1) * 512], in_=tflat[:, k * 512:(k + 1) * 512], func=AF.Exp),
                      reads=["tblA"], writes=["tblA"])
        acc_ring = Ring([0, 2])
        scr_ring = Ring([4, 5, 6, 7])
        pt_ring = Ring([0, 1, 2, 3])
        nrm_ring = Ring([0, 1])
        pv_pending = []

        def flush_pv(keep=0):
            while len(pv_pending) > keep:
                pv_pending.pop(0)()

        def zero_acc(b):
            sc.op("dve", lambda e: e.memset(ps[:, b * 512:(b + 2) * 512], 0.0), writes=[("ps", b), ("ps", b + 1)])

        def na_unit(QB, hp, ms):
            ab = acc_ring.next()
            OA, DA = ab, ab + 1
            jobs = _na_jobs(QB)
            banks_jobs = []
            cur = []
            ncol = 0
            for a0, Rs in jobs:
                n = 64 * len(Rs)
                if ncol + n > 512:
                    banks_jobs.append(cur)
                    cur = []
                    ncol = 0
                cur.append((a0, Rs, ncol))
                ncol += n
            if cur:
                banks_jobs.append(cur)
            for bj in banks_jobs:
                sbs = [scr_ring.next(), scr_ring.next()]
                pts = [pt_ring.next(), pt_ring.next()]
                tot = bj[-1][2] + 64 * len(bj[-1][1])
                g_bias, g_s = [[], []], [[], []]
                for hh in range(2):
                    h = 2 * hp + hh
                    pbase = 64 * hh
                    sb = sbs[hh]
                    for a0, Rs, co in bj:
                        rc = co
                        for (R0, st, n) in _pruns(Rs):
                            o3 = ps[:, sb * 512 + rc:sb * 512 + rc + 64 * n].rearrange("p (n c) -> p n c", c=64)
                            dri0 = 6 - (a0 - R0)
                            brhs = tblA[:, h, dri0:dri0 + st * (n - 1) + 1:st, :]
                            qv = QA_T[pbase:pbase + 64, hp, :].rearrange("p (r c) -> p r c", c=64)[:, R0:R0 + st * (n - 1) + 1:st, :]
                            g_bias[hh].append((o3, brhs))
                            g_s[hh].append((o3, qv, KA_T[pbase:pbase + 64, hp, a0 * 64:a0 * 64 + 128]))
                            rc += 64 * n
                for hh in range(2):
                    for gi, (o3, brhs) in enumerate(g_bias[hh]):
                        sc.op("pe", lambda e, o3=o3, brhs=brhs, gi=gi: e.matmul(
                            o3, lhsT=ident_bf, rhs=brhs, start=(gi == 0), stop=False, skip_group_check=True),
                            reads=["tblA", "cbf"], writes=[("ps", sbs[hh])])
                for gi in range(len(g_s[0])):
                    for hh in range(2):
                        (o3, qv, kl) = g_s[hh][gi]
                        sc.op("pe", lambda e, o3=o3, qv=qv, kl=kl, tp_=(64 * hh, 0): e.matmul(
                            o3, lhsT=kl, rhs=qv, start=False, stop=True, tile_position=tp_, skip_group_check=True),
                            reads=[], writes=[("ps", sbs[hh])])
                flush_pv(2)
                for hh in range(2):
                    sc.op("act", lambda e, sb=sbs[hh], pt=pts[hh], tot=tot: e.activation(out=PT[pt][:, 0:tot], in_=bank(sb, tot), func=AF.Exp),
                          reads=[("ps", sbs[hh])], writes=[("PT", pts[hh])])
                for hh in range(2):
                    def pv(bj=bj, pt=pts[hh], h=2 * hp + hh, pbase=64 * hh, OA=OA, DA=DA):
                        for a0, Rs, co in bj:
                            rc = co
                            vl = VA[:, a0 // 2, h * 64:(h + 1) * 64] if a0 % 2 == 0 else VAo[:, (a0 - 1) // 2, h * 64:(h + 1) * 64]
                            for (R0, st, n) in _pruns(Rs):
                                Rl = R0 - 8 * QB
                                pos0 = (R0 % 2) * 4 + Rl // 2
                                oa = ps[pbase:pbase + 64, OA * 512 + pos0 * 64:OA * 512 + (pos0 + n) * 64]
                                da = ps[pbase:pbase + 64, DA * 512 + pos0 * 64:DA * 512 + (pos0 + n) * 64]
                                p3 = PT[pt][:, rc:rc + 64 * n]
                                for (o_, l_, bk) in ((oa, vl, OA), (da, ones_bf[:, 0:64], DA)):
                                    sc.op("pe", lambda e, o_=o_, l_=l_, p3=p3, tp_=(0, pbase): e.matmul(
                                        o_, lhsT=l_, rhs=p3, start=False, stop=False, skip_group_check=True, tile_position=tp_),
                                        reads=[("PT", pt), "ones", "VAo"], writes=[("ps", bk)])
                                rc += 64 * n
                    pv_pending.append(pv)
            flush_pv()
            nr = nrm_ring.next()
            qsl = slice(QB * 512, (QB + 1) * 512)
            if NA_RECIP_DVE:
                sc.op("dve", lambda e: e.reciprocal(out=rdenb[nr], in_=bank(DA)), reads=[("ps", DA)], writes=[("rden", nr)])
            else:
                sc.op("act", lambda e: e.activation(out=rdenb[nr], in_=bank(DA), func=AF.Ln), reads=[("ps", DA)], writes=[("rden", nr)])
                sc.op("act", lambda e: e.activation(out=rdenb[nr], in_=rdenb[nr], func=AF.Exp, scale=-1.0),
                      reads=[("rden", nr)], writes=[("rden", nr)])
            sc.op("dve", lambda e: e.tensor_tensor(out=g2b[nr].rearrange("p (i par c) -> p par i c", par=2, c=64),
                                                   in0=bank(OA).rearrange("p (par i c) -> p par i c", par=2, c=64),
                                                   in1=rdenb[nr].rearrange("p (par i c) -> p par i c", par=2, c=64), op=ALU.mult),
                  reads=[("ps", OA), ("rden", nr)], writes=[("g2", nr)])
            sc.op("pool", lambda e: e.tensor_tensor(out=mixT[ms][:, hp, :], in0=g2b[nr], in1=gateA[:, hp, qsl], op=ALU.mult),
                  reads=[("g2", nr)], writes=[("mix", ms, hp)])
            if not (QB == 3 and hp == 3):
                zero_acc(ab)

        def swa_unit(n, ms):
            ab = acc_ring.next()
            OB, DB = ab, ab + 1
            nl = n % 4
            for kb in (n - 1, n, n + 1):
                if kb < 0 or kb > 15:
                    continue
                sbs = [scr_ring.next(), scr_ring.next()]
                pts = [pt_ring.next(), pt_ring.next()]
                if kb != n:
                    kind = 0 if kb < n else 1
                    for g in range(2):
                        sc.op("pe", lambda e, sb=sbs[g], kind=kind: e.matmul(bank(sb), lhsT=ident_bf, rhs=masks_bf[:, kind, :], start=True, stop=False),
                              reads=["cbf"], writes=[("ps", sbs[g])])
                for j in range(4):
                    for g in range(2):
                        pb = 64 * g
                        st_ = (kb == n)
                        sp_ = (kb == n) or (j == 3)
                        sc.op("pe", lambda e, sb=sbs[g], j=j, kb=kb, pb=pb, st_=st_, sp_=sp_: e.matmul(
                            ps[:, sb * 512 + j * 128:sb * 512 + (j + 1) * 128], lhsT=KB_T[pb:pb + 64, kb * 128:(kb + 1) * 128],
                            rhs=QB_T[pb:pb + 64, j, n * 128:(n + 1) * 128], start=st_, stop=sp_, tile_position=(pb, 0)),
                            reads=[], writes=[("ps", sbs[g])])
                flush_pv(2)
                for g in range(2):
                    sc.op("act", lambda e, sb=sbs[g], pt=pts[g]: e.activation(out=PT[pt], in_=bank(sb), func=AF.Exp, scale=0.125),
                          reads=[("ps", sbs[g])], writes=[("PT", pts[g])])
                for g in range(2):
                    def pv(pt=pts[g], kb=kb, pb=64 * g, OB=OB, DB=DB):
                        sc.op("pe", lambda e: e.matmul(ps[pb:pb + 64, OB * 512:OB * 512 + 512], lhsT=VB[:, kb, pb:pb + 64], rhs=PT[pt],
                                                       start=False, stop=False, skip_group_check=True, tile_position=(0, pb)),
                              reads=[("PT", pt)], writes=[("ps", OB)])
                        sc.op("pe", lambda e: e.matmul(ps[pb:pb + 64, DB * 512:DB * 512 + 512], lhsT=ones_bf[:, 0:64], rhs=PT[pt],
                                                       start=False, stop=False, skip_group_check=True, tile_position=(0, pb)),
                              reads=[("PT", pt)], writes=[("ps", DB)])
                    pv_pending.append(pv)
            flush_pv()
            nr = nrm_ring.next()
            d3 = rdenb[nr].rearrange("p (j q) -> p j q", j=4)
            sc.op("dve", lambda e: e.tensor_tensor(out=d3, in0=bank(DB).rearrange("p (j q) -> p j q", j=4),
                                                   in1=sinkE.unsqueeze(2).broadcast_to([128, 4, 128]), op=ALU.add),
                  reads=[("ps", DB), "sinkE"], writes=[("rden", nr)])
            if n % 4 < 2:
                sc.op("dve", lambda e: e.reciprocal(out=rdenb[nr], in_=rdenb[nr]), reads=[("rden", nr)], writes=[("rden", nr)])
            else:
                sc.op("act", lambda e: e.activation(out=rdenb[nr], in_=rdenb[nr], func=AF.Ln), reads=[("rden", nr)], writes=[("rden", nr)])
                sc.op("act", lambda e: e.activation(out=rdenb[nr], in_=rdenb[nr], func=AF.Exp, scale=-1.0),
                      reads=[("rden", nr)], writes=[("rden", nr)])
            sc.op("dve", lambda e: e.tensor_tensor(out=g2b[nr], in0=bank(OB), in1=rdenb[nr], op=ALU.mult),
                  reads=[("ps", OB), ("rden", nr)], writes=[("g2", nr)])
            sc.op("pool", lambda e: e.tensor_tensor(out=mixT[ms][:, 4:8, nl * 128:(nl + 1) * 128],
                                                    in0=g2b[nr].rearrange("p (j q) -> p j q", j=4),
                                                    in1=gateB[:, :, n * 128:(n + 1) * 128], op=ALU.mult),
                  reads=[("g2", nr)], writes=[("mix", ms, 4 + nl)])
            if n != 15:
                zero_acc(ab)

        def load_xres(i):
            sl = i % 4
            sc.op("sp", lambda e: e.dma_start(out=xf[sl], in_=x_d[i * 128:(i + 1) * 128, :]), writes=[("xf", sl)], dma=("xf", sl))

        out_back = []

        def out_unit(i, ms):
            il = i % 4
            sl = i % 4
            b = scr_ring.next_pair()
            for nb in range(2):
                for c in range(8):
                    sc.op("pe", lambda e, nb=nb, c=c: e.matmul(bank(b + nb), lhsT=mixT[ms][:, c, il * 128:(il + 1) * 128],
                                                              rhs=Wout[:, c, nb * 512:(nb + 1) * 512], start=(c == 0), stop=(c == 7)),
                          reads=[("mix", ms, cc) for cc in range(4)] + [("mix", ms, 4 + il), "Wout"],
                          writes=[("ps", b + nb)])
            yps = ps[:, b * 512:b * 512 + 1024]
            sc.op("dve", lambda e: e.scalar_tensor_tensor(out=xf[sl], in0=xf[sl], scalar=ALPHA, in1=yps, op0=ALU.mult, op1=ALU.add),
                  reads=[("ps", b), ("ps", b + 1), ("xf", sl)], writes=[("xf", sl)])
            sc.op("dve", lambda e: e.bn_stats(out=stt[sl][:, 0:6], in_=xf[sl][:, 0:512]), reads=[("xf", sl)], writes=[("st", sl, 0)])
            sc.op("dve", lambda e: e.bn_stats(out=stt[sl][:, 6:12], in_=xf[sl][:, 512:1024]), reads=[("xf", sl)], writes=[("st", sl, 1)])
            sc.op("dve", lambda e: e.bn_aggr(out=mvt[sl], in_=stt[sl]), reads=[("st", sl, 0), ("st", sl, 1)], writes=[("mv", sl)])
            sc.op("dve", lambda e: e.tensor_scalar_add(out=rst[sl][:, 1:2], in0=mvt[sl][:, 1:2], scalar1=LN_EPS),
                  reads=[("mv", sl)], writes=[("rs", sl, 1)])
            sc.op("act", lambda e: e.activation(out=rst[sl][:, 1:2], in_=rst[sl][:, 1:2], func=AF.Ln),
                  reads=[("rs", sl, 1)], writes=[("rs", sl, 1)])
            sc.op("act", lambda e: e.activation(out=rst[sl][:, 0:1], in_=rst[sl][:, 1:2], func=AF.Exp, scale=-0.5),
                  reads=[("rs", sl, 1)], writes=[("rs", sl, 0)])

            def back():
                if i >= 12:
                    sc.op("dve", lambda e: e.scalar_tensor_tensor(out=xf[sl], in0=xf[sl], scalar=mvt[sl][:, 0:1], in1=gbt[:, 0:D],
                                                                  op0=ALU.subtract, op1=ALU.mult),
                          reads=[("xf", sl), ("mv", sl), "gbt"], writes=[("xf", sl)])
                    sc.op("dve", lambda e: e.scalar_tensor_tensor(out=xf[sl], in0=xf[sl], scalar=rst[sl][:, 0:1], in1=gbt[:, D:2 * D],
                                                                  op0=ALU.mult, op1=ALU.add),
                          reads=[("xf", sl), ("rs", sl, 0), "gbt"], writes=[("xf", sl)])
                else:
                    sc.op("dve", lambda e: e.tensor_scalar(out=xf[sl], in0=xf[sl], scalar1=mvt[sl][:, 0:1], scalar2=rst[sl][:, 0:1],
                                                           op0=ALU.subtract, op1=ALU.mult),
                          reads=[("xf", sl), ("mv", sl), ("rs", sl, 0)], writes=[("xf", sl)])
                    sc.op("dve" if i >= 12 else "pool", lambda e: e.tensor_tensor(out=xf[sl], in0=xf[sl], in1=gbt[:, 0:D], op=ALU.mult),
                          reads=[("xf", sl), "gbt"], writes=[("xf", sl)])
                    sc.op("pool", lambda e: e.tensor_tensor(out=xf[sl], in0=xf[sl], in1=gbt[:, D:2 * D], op=ALU.add),
                          reads=[("xf", sl), "gbt"], writes=[("xf", sl)])
                sc.op("sp", lambda e: e.dma_start(out=out_d[i * 128:(i + 1) * 128, :], in_=xf[sl]),
                      reads=[("xf", sl)], dma=("xo", sl))
            while out_back:
                out_back.pop(0)()
            out_back.append(back)

        zero_acc(0)
        zero_acc(2)
        for QB in range(4):
            ms = QB % 2
            if QB == 0:
                for n in range(4):
                    swa_unit(n, ms)
                for i in range(4):
                    load_xres(i)
                for hp in range(4):
                    na_unit(QB, hp, ms)
                if stop == "B0":
                    dump(mixT[0].rearrange("p a b -> p (a b)"), 4096)
                for i in range(4):
                    out_unit(i, ms)
            else:
                for i in range(4 * QB, 4 * QB + 4):
                    load_xres(i)
                for k in range(4):
                    na_unit(QB, k, ms)
                    swa_unit(4 * QB + k, ms)
                for n in range(4 * QB, 4 * QB + 4):
                    out_unit(n, ms)
            while out_back:
                out_back.pop(0)()
            if stop is not None and stop.startswith("CQ") and QB == int(stop[2:]):
                dump(mixT[0].rearrange("p a b -> p (a b)"), 4096)

    try:
        body()
    except _Stop:
        pass

    fin = [(sk, v) for sk, v in sc.dma_cnt.items() if isinstance(sk[1], tuple) and sk[1][0] == "xo"]
    sc.streams["sp"].append((fin, None, None))

    with ExitStack() as es:
        sems = {}
        for e in ENGS:
            sems[e] = es.enter_context(nc.semaphore("s_" + e))
        for k, sk in enumerate(sc.dma_cnt.keys()):
            sems[sk] = es.enter_context(nc.semaphore("d_%d" % k))
        block = es.enter_context(nc.Block())

        def emit(eng_handle, name):
            for waits, fn, inc in sc.streams[name]:
                for sk, v in waits:
                    eng_handle.wait_ge(sems[sk], v)
                if fn is not None:
                    ins = fn(eng_handle)
                    ins.then_inc(sems[inc[0]], inc[1])

        @block.tensor
        def _(t):
            emit(t, "pe")

        @block.scalar
        def _(a):
            emit(a, "act")

        @block.vector
        def _(v):
            emit(v, "dve")

        @block.gpsimd
        def _(g):
            emit(g, "pool")

        @block.sync
        def _(s):
            emit(s, "sp")
    return nc


_NC_CACHE = {}


def kernel(x, w_in, rel_pos_bias, sink_logits, w_out, ln_gain, ln_bias):
    x = np.asarray(x, dtype=np.float32)
    w_in = np.asarray(w_in, dtype=np.float32)[0]
    w_out = np.asarray(w_out, dtype=np.float32)[0]
    rpb = np.asarray(rel_pos_bias, dtype=np.float32)[0]
    sink = np.asarray(sink_logits, dtype=np.float32)[0]
    gain = np.asarray(ln_gain, dtype=np.float32)[0]
    bias = np.asarray(ln_bias, dtype=np.float32)[0]

    w_in_p = np.ascontiguousarray(w_in[:, _w_in_perm()])
    w_out_p = np.ascontiguousarray(w_out[_w_out_perm(), :])
    tbl_a = _na_table(rpb)
    cmat, cosT, sinT = _consts()
    sink_lay = np.ascontiguousarray(np.repeat(sink.reshape(2, 4), 64, axis=0))
    gb = np.ascontiguousarray(np.tile(np.concatenate([gain, bias])[None, :], (128, 1)))

    if "nc" not in _NC_CACHE:
        _NC_CACHE["nc"] = build_nc()
    nc = _NC_CACHE["nc"]
    shared = {"w_in_p": w_in_p, "w_out_p": w_out_p, "tbl_a": tbl_a, "cmat": cmat, "cos_t": cosT, "sin_t": sinT,
              "sink_lay": sink_lay, "gain_bias": gb}
    in_maps = []
    for b in range(N_CORES):
        m = dict(shared)
        m["x"] = np.ascontiguousarray(x[b])
        in_maps.append(m)
    res = run_bass_kernel_spmd(nc, in_maps, core_ids=list(range(N_CORES)))
    out = np.stack([np.asarray(r["out"], dtype=np.float32).reshape(S, D) for r in res.results], axis=0)
    return out
```

```python
import numpy as np
from contextlib import ExitStack
import concourse.bass as bass
import concourse.mybir as mybir
from concourse.bass_utils import run_bass_kernel_spmd

F32 = mybir.dt.float32
BF = mybir.dt.bfloat16
AF = mybir.ActivationFunctionType
ALU = mybir.AluOpType

S = 2048
D = 1024
NEG = -30000.0
ALPHA = float(2.0 ** 0.25)
LN_EPS = 1e-5
N_CORES = 8
import os
ROT_ADD_ENG = os.environ.get('ROT_ADD_ENG', 'pool')
SAME_ENG_SYNC = int(os.environ.get('SAME_ENG_SYNC', '1'))
GROUP_S = int(os.environ.get('GROUP_S', '1'))
GROUP_PV = int(os.environ.get('GROUP_PV', '0'))
LOOKAHEAD = int(os.environ.get('LOOKAHEAD', '3'))
BIAS_PE = int(os.environ.get('BIAS_PE', '1'))
PREFETCH = int(os.environ.get('PREFETCH', '0'))
NA_RECIP_DVE = int(os.environ.get('NA_RECIP_DVE', '1'))
ROT_DBG = int(os.environ.get('ROT_DBG', '5'))

FM_TYPES = [("KA", k) for k in range(4)] + [("QA", k) for k in range(4)] + [("QB", k) for k in range(4)] + \
           [("KB", 0)] + [("ZA", k) for k in range(4)] + [("ZB", k) for k in range(4)]
N_FM = len(FM_TYPES)
FM_COLS = N_FM * 128
V_COLS = 640


def _w_in_perm():
    cols = []
    qa, ka, va, za, qb, kb, vb, zb = 0, 512, 1024, 1536, 2048, 2560, 2688, 2816
    for t, k in FM_TYPES:
        if t == "KA":
            cols += list(range(ka + 128 * k, ka + 128 * (k + 1)))
        elif t == "QA":
            cols += list(range(qa + 128 * k, qa + 128 * (k + 1)))
        elif t == "ZA":
            cols += list(range(za + 128 * k, za + 128 * (k + 1)))
        elif t == "QB":
            cols += list(range(qb + 64 * k, qb + 64 * (k + 1))) + list(range(qb + 64 * (4 + k), qb + 64 * (5 + k)))
        elif t == "ZB":
            cols += list(range(zb + 64 * k, zb + 64 * (k + 1))) + list(range(zb + 64 * (4 + k), zb + 64 * (5 + k)))
        elif t == "KB":
            cols += list(range(kb, kb + 128))
    cols += list(range(va, va + 512)) + list(range(vb, vb + 128))
    return np.array(cols, dtype=np.int64)


def _w_out_perm():
    rows = list(range(512))
    for j in range(4):
        rows += list(range(512 + 64 * j, 512 + 64 * (j + 1))) + list(range(512 + 64 * (4 + j), 512 + 64 * (5 + j)))
    return np.array(rows, dtype=np.int64)


def _na_table(rpb):
    jr = np.arange(128) // 64
    jc = np.arange(128) % 64
    qc = np.arange(64)
    cs = np.clip(qc - 8, 0, 48)
    colvalid = (jc[:, None] >= cs[None, :]) & (jc[:, None] < cs[None, :] + 16)
    dc_idx = np.clip(jc[:, None] - qc[None, :] + 15, 0, 30)
    tbl = np.full((128, 8, 14, 64), NEG, dtype=np.float32)
    for dri in range(14):
        dr0 = 6 - dri
        dr_idx = dr0 + jr + 7
        g = rpb[:, dr_idx[:, None], dc_idx]
        g = np.transpose(g, (1, 0, 2))
        tbl[:, :, dri, :] = np.where(colvalid[:, None, :], g, np.float32(NEG))
    return np.ascontiguousarray(tbl.reshape(128, 8 * 14 * 64))


def _consts():
    ident = np.eye(128, dtype=np.float32)
    swap = np.zeros((128, 128), np.float32)
    for m in range(128):
        swap[(m + 64) % 128, m] = 1.0
    perm = np.zeros((128, 128), np.float32)
    for p in range(128):
        if p % 64 < 32:
            perm[p + 32, p] = -1.0
        else:
            perm[p - 32, p] = 1.0
    j = np.arange(128)[:, None]
    q = np.arange(128)[None, :]
    mprev = np.where(j >= q, 0.0, NEG).astype(np.float32)
    mnext = np.where(j <= q, 0.0, NEG).astype(np.float32)
    masks = np.concatenate([np.tile(mprev, (1, 4)), np.tile(mnext, (1, 4))], axis=1)
    cmat = np.concatenate([ident, swap, perm, masks], axis=1)
    inv_freq = (np.float32(10000.0) ** (-(np.arange(0, 64, 2, dtype=np.float32)) / np.float32(64))).astype(np.float32)
    ang = (np.arange(S, dtype=np.float32)[:, None] * inv_freq[None, :]).astype(np.float32)
    cosT = np.cos(ang).astype(np.float32).T
    sinT = np.sin(ang).astype(np.float32).T
    cosT = np.ascontiguousarray(np.tile(cosT, (4, 1)))
    sinT = np.ascontiguousarray(np.tile(sinT, (4, 1)))
    return np.ascontiguousarray(cmat), cosT, sinT


ENGS = ("pe", "act", "dve", "pool", "sp")


class Sched:
    def __init__(self):
        self.streams = {e: [] for e in ENGS}
        self.cnt = {e: 0 for e in ENGS}
        self.dma_cnt = {}
        self.lastw = {}
        self.readers = {}
        self.waited = {e: {} for e in ENGS}

    def _deps(self, reads, writes):
        deps = {}

        def add(ev):
            if ev is None:
                return
            sk, v = ev
            if deps.get(sk, 0) < v:
                deps[sk] = v
        for r in reads:
            add(self.lastw.get(r))
        for w in writes:
            add(self.lastw.get(w))
            for sk, v in self.readers.get(w, {}).items():
                add((sk, v))
        return deps

    def op(self, eng, fn, reads=(), writes=(), dma=None):
        writes = list(writes) + [r for r in reads if isinstance(r, tuple) and r[0] == "ps"]
        if eng == "pool" and dma is not None:
            writes.append("poolq")
        reads = [r for r in reads if not (isinstance(r, tuple) and r[0] == "ps")]
        deps = self._deps(reads, writes)
        waits = []
        for sk, v in deps.items():
            if sk == eng and (eng == "pe" or not SAME_ENG_SYNC):
                continue
            if self.waited[eng].get(sk, 0) >= v:
                continue
            self.waited[eng][sk] = v
            waits.append((sk, v))
        if dma is None:
            self.cnt[eng] += 1
            ev = (eng, self.cnt[eng])
            inc = (eng, 1)
        else:
            sk = ("dma", dma)
            self.dma_cnt[sk] = self.dma_cnt.get(sk, 0) + 16
            ev = (sk, self.dma_cnt[sk])
            inc = (sk, 16)
        self.streams[eng].append((waits, fn, inc))
        for r in reads:
            d = self.readers.setdefault(r, {})
            if d.get(ev[0], 0) < ev[1]:
                d[ev[0]] = ev[1]
        for w in writes:
            self.lastw[w] = ev
            self.readers[w] = {}
        return ev

    def barrier(self):
        cur = {e: self.cnt[e] for e in ENGS if self.cnt[e] > 0}
        cur.update(self.dma_cnt)
        for e in ENGS:
            waits = []
            for sk, v in cur.items():
                if sk == e:
                    continue
                if self.waited[e].get(sk, 0) >= v:
                    continue
                self.waited[e][sk] = v
                waits.append((sk, v))
            if waits:
                self.streams[e].append((waits, None, None))


class Ring:
    def __init__(self, items):
        self.items = list(items)
        self.i = 0

    def next(self):
        v = self.items[self.i % len(self.items)]
        self.i += 1
        return v

    def next_pair(self):
        if self.i % 2 == 1:
            self.i += 1
        a = self.items[self.i % len(self.items)]
        self.i += 2
        return a


def _runs(Rs):
    out = []
    i = 0
    while i < len(Rs):
        if i + 1 < len(Rs):
            st = Rs[i + 1] - Rs[i]
            j = i + 1
            while j + 1 < len(Rs) and Rs[j + 1] - Rs[j] == st:
                j += 1
            out.append((Rs[i], st, j - i + 1))
            i = j + 1
        else:
            out.append((Rs[i], 1, 1))
            i += 1
    return out


def _pruns(Rs):
    out = []
    for par in (0, 1):
        sub = [R for R in Rs if R % 2 == par]
        i = 0
        while i < len(sub):
            j = i
            while j + 1 < len(sub) and sub[j + 1] - sub[j] == 2:
                j += 1
            out.append((sub[i], 2, j - i + 1))
            i = j + 1
    return out


def _na_jobs(QB):
    by = {}
    for R in range(8 * QB, 8 * QB + 8):
        r0 = min(max(R - 4, 0), 24)
        for a0 in (r0, r0 + 2, r0 + 4, r0 + 6):
            by.setdefault(a0, []).append(R)
    return sorted(by.items())


def build_nc(stop=None):
    nc = bass.Bass("TRN2", target_bir_lowering=False)
    x_d = nc.dram_tensor("x", [S, D], F32, kind="ExternalInput").ap()
    win_d = nc.dram_tensor("w_in_p", [D, FM_COLS + V_COLS], F32, kind="ExternalInput").ap()
    wout_d = nc.dram_tensor("w_out_p", [D, D], F32, kind="ExternalInput").ap()
    tbl_d = nc.dram_tensor("tbl_a", [128, 8 * 14 * 64], F32, kind="ExternalInput").ap()
    cmat_d = nc.dram_tensor("cmat", [128, 1408], F32, kind="ExternalInput").ap()
    cos_d = nc.dram_tensor("cos_t", [128, S], F32, kind="ExternalInput").ap()
    sin_d = nc.dram_tensor("sin_t", [128, S], F32, kind="ExternalInput").ap()
    sink_d = nc.dram_tensor("sink_lay", [128, 4], F32, kind="ExternalInput").ap()
    gb_d = nc.dram_tensor("gain_bias", [128, 2 * D], F32, kind="ExternalInput").ap()
    out_d = nc.dram_tensor("out", [S, D], F32, kind="ExternalOutput").ap()
    dbg_d = None
    if stop is not None:
        dbg_d = nc.dram_tensor("dbg", [128, 53248], BF, kind="ExternalOutput").ap()

    ARENA_BYTES = 211456
    arena = nc.alloc_sbuf_tensor("arena", [128, ARENA_BYTES // 2], BF)
    ps = nc.alloc_psum_tensor("ps", [128, 4096], F32)

    class Carver:
        def __init__(self, base=0):
            self.off = base
            self.hi = base

        def get(self, fshape, dt):
            n = int(np.prod(fshape))
            nb = n * (4 if dt == F32 else 2)
            nb_al = (nb + 63) // 64 * 64
            off = self.off
            self.off += nb_al
            self.hi = max(self.hi, self.off)
            assert self.off <= ARENA_BYTES, f"SBUF arena overflow {self.off}"
            a = arena[:, off // 2:(off + nb) // 2]
            if dt == F32:
                a = a.bitcast(F32)
            if len(fshape) == 2:
                a = a.rearrange("p (a b) -> p a b", b=fshape[1])
            elif len(fshape) == 3:
                a = a.rearrange("p (a b c) -> p a b c", b=fshape[1], c=fshape[2])
            return a

    cv = Carver()
    QA_T = cv.get([4, S], BF)
    KA_T = cv.get([4, S], BF)
    gateA = cv.get([4, S], BF)
    gateB = cv.get([4, S], BF)
    QB_T = cv.get([4, S], BF)
    KB_T = cv.get([S], BF)
    VA = cv.get([16, 512], BF)
    VB = cv.get([16, 128], BF)
    cbf = cv.get([1408], BF)
    ident_f = cv.get([128], F32)
    ones_bf = cv.get([128], BF)
    sinkE = cv.get([4], F32)
    sink_raw = cv.get([4], F32)
    xf = [cv.get([D], F32) for _ in range(4)]
    ident_bf = cbf[:, 0:128]
    swap_bf = cbf[:, 128:256]
    perm_bf = cbf[:, 256:384]
    masks_bf = cbf[:, 384:1408].rearrange("p (k n) -> p k n", k=2)
    phase_base = cv.off
    ca = Carver(phase_base)
    xT = ca.get([8, S], BF)
    wb = [ca.get([8, 256], BF) for _ in range(3)]
    off_wv = ca.off
    wv = ca.get([8, V_COLS], BF)
    cosT = ca.get([S], F32)
    sinT = ca.get([S], F32)
    qraw = [ca.get([512], BF) for _ in range(2)]
    t2b = [ca.get([512], F32) for _ in range(2)]
    t3b = [ca.get([512], F32) for _ in range(2)]
    cb = Carver(phase_base)
    mixT = [cb.get([8, 512], BF) for _ in range(2)]
    PT = [cb.get([512], BF) for _ in range(4)]
    rdenb = [cb.get([512], F32) for _ in range(2)]
    g2b = [cb.get([512], F32) for _ in range(2)]
    stt = [cb.get([12], F32) for _ in range(4)]
    mvt = [cb.get([2], F32) for _ in range(4)]
    rst = [cb.get([2], F32) for _ in range(4)]
    VAo = cb.get([15, 512], BF)
    assert cb.off <= off_wv, (cb.off, off_wv)
    cb.off = off_wv
    Wout = cb.get([8, D], BF)
    tblA = cb.get([8, 14, 64], BF)
    gbt = cb.get([2 * D], F32)

    def bank(b, w=512):
        return ps[:, b * 512:b * 512 + w]

    sc = Sched()

    class _Stop(Exception):
        pass

    def dump(ap_src, ncols):
        sc.barrier()
        sc.op("sp", lambda e: e.dma_start(out=dbg_d[:, 0:ncols], in_=ap_src), dma=("xo", 0))
        raise _Stop()

    def body():

        sc.op("sp", lambda e: e.dma_start(out=ident_f, in_=cmat_d[:, 0:128]), writes=["ident_f"], dma="identf")
        sc.op("sp", lambda e: e.dma_start(out=sink_raw, in_=sink_d), writes=["sink_raw"], dma="sink")
        sc.op("pool", lambda e: e.memset(ones_bf, 1.0), writes=["ones"])
        sc.op("act", lambda e: e.activation(out=sinkE, in_=sink_raw, func=AF.Exp), reads=["sink_raw"], writes=["sinkE"])

        if stop == "setup":
            dump(cbf, 1408)
        def load_x(i):
            sl = i % 4
            sc.op("sp", lambda e, i=i, sl=sl: e.dma_start(out=xf[sl], in_=x_d[i * 128:(i + 1) * 128, :]),
                  writes=[("xf", sl)], dma=("xf", sl))
        for i in range(4):
            load_x(i)

        wblocks = []
        c0 = 0
        while c0 < N_FM:
            n = min(2, N_FM - c0)
            wblocks.append((c0, n))
            c0 += n
        wslot_of_chunk = {}

        def load_wblock(bi, after=()):
            c0, n = wblocks[bi]
            sl = bi % 3
            src = win_d[:, c0 * 128:(c0 + n) * 128].rearrange("(kc p) n -> p kc n", p=128)
            sc.op("pool", lambda e, sl=sl, n=n, src=src: e.dma_start(out=wb[sl][:, :, 0:n * 128], in_=src),
                  reads=list(after), writes=[("wb", sl)], dma=("wb", sl))
            for k in range(n):
                wslot_of_chunk[c0 + k] = (sl, k)

        sc.op("pool", lambda e: e.dma_start(out=wv, in_=win_d[:, FM_COLS:FM_COLS + V_COLS].rearrange("(kc p) n -> p kc n", p=128)),
              writes=["wv"], dma="wv")
        sc.op("pool", lambda e: e.dma_start(out=cbf, in_=cmat_d), writes=["cbf"], dma="cbf")
        if stop == "A0":
            dump(wv.rearrange("p a b -> p (a b)"), 8 * V_COLS)

        fm_ring = Ring([0, 1, 2, 3])
        perm_ring = Ring([4, 5])
        rot_i = [0]
        pending = []
        v_next = [0]

        def emit_v_job():
            i = v_next[0]
            if i >= 16:
                return
            v_next[0] += 1
            for kc in range(8):
                sc.op("pe", lambda e, kc=kc, i=i: e.matmul(bank(6), lhsT=xT[:, kc, i * 128:(i + 1) * 128], rhs=wv[:, kc, 0:512],
                                                            start=(kc == 0), stop=(kc == 7)),
                      reads=[("xT", i), "wv"], writes=[("ps", 6)])
            for kc in range(8):
                sc.op("pe", lambda e, kc=kc, i=i: e.matmul(bank(7, 128), lhsT=xT[:, kc, i * 128:(i + 1) * 128], rhs=wv[:, kc, 512:640],
                                                            start=(kc == 0), stop=(kc == 7)),
                      reads=[("xT", i), "wv"], writes=[("ps", 7)])
            sc.op("dve", lambda e, i=i: e.tensor_copy(out=VA[:, i, :], in_=bank(6)), reads=[("ps", 6)], writes=[("VA", i)])
            sc.op("act", lambda e, i=i: e.copy(out=VB[:, i, :], in_=bank(7, 128)), reads=[("ps", 7)], writes=[("VB", i)])

        for i in range(16):
            sl = i % 4
            b = 2 * (i % 2)
            for kc in range(8):
                sc.op("pe", lambda e, b=b, kc=kc, sl=sl: e.transpose(out=ps[:, b * 512 + kc * 128:b * 512 + (kc + 1) * 128],
                                                                      in_=xf[sl][:, kc * 128:(kc + 1) * 128], identity=ident_f),
                      reads=[("xf", sl), "ident_f"], writes=[("ps", b), ("ps", b + 1)])
            src = ps[:, b * 512:b * 512 + 1024].rearrange("p (k t) -> p k t", k=8)
            dst = xT[:, :, i * 128:(i + 1) * 128]
            if i % 2 == 0:
                sc.op("dve", lambda e, src=src, dst=dst: e.tensor_copy(out=dst, in_=src),
                      reads=[("ps", b), ("ps", b + 1)], writes=[("xT", i)])
            else:
                sc.op("act", lambda e, src=src, dst=dst: e.copy(out=dst, in_=src),
                      reads=[("ps", b), ("ps", b + 1)], writes=[("xT", i)])
            if i + 4 < 16:
                load_x(i + 4)
            if i == 11:
                sc.op("sp", lambda e: e.dma_start(out=cosT, in_=cos_d), writes=["cos"], dma="cos")
                sc.op("sp", lambda e: e.dma_start(out=sinT, in_=sin_d), writes=["sin"], dma="sin")
            if i == 11:
                load_wblock(0, after=[("xT", 11)])
                load_wblock(1, after=[("xT", 11)])
            if i >= 1:
                emit_v_job()
        if stop == "xT":
            dump(xT.rearrange("p a b -> p (a b)"), 16384)

        for c in range(N_FM):
            typ, k = FM_TYPES[c]
            bi = c // 2
            if c % 2 == 0 and bi + 2 < len(wblocks):
                pass
            sl, kk = wslot_of_chunk[c]
            for tb in range(4):
                b = fm_ring.next()
                tsl = slice(tb * 512, (tb + 1) * 512)
                for kc in range(8):
                    sc.op("pe", lambda e, b=b, kc=kc, sl=sl, kk=kk, tsl=tsl: e.matmul(
                        bank(b), lhsT=wb[sl][:, kc, kk * 128:(kk + 1) * 128], rhs=xT[:, kc, tsl], start=(kc == 0), stop=(kc == 7)),
                        reads=[("xT", 4 * tb), ("xT", 4 * tb + 1), ("xT", 4 * tb + 2), ("xT", 4 * tb + 3), ("wb", sl)], writes=[("ps", b)])
                while pending:
                    pending.pop(0)()
                if typ == "KA":
                    sc.op("dve", lambda e, b=b, k=k, tsl=tsl: e.tensor_copy(out=KA_T[:, k, tsl], in_=bank(b)),
                          reads=[("ps", b)], writes=[("KA", k, tb)])
                elif typ == "QA":
                    sc.op("act", lambda e, b=b, k=k, tsl=tsl: e.mul(out=QA_T[:, k, tsl], in_=bank(b), mul=0.125),
                          reads=[("ps", b)], writes=[("QA", k, tb)])
                elif typ in ("ZA", "ZB"):
                    dst = (gateA if typ == "ZA" else gateB)[:, k, tsl]
                    sc.op("act", lambda e, b=b, dst=dst: e.activation(out=dst, in_=bank(b), func=AF.Silu),
                          reads=[("ps", b)], writes=[(typ, k, tb)])
                else:
                    r = rot_i[0] % 2
                    rot_i[0] += 1
                    dst = QB_T[:, k, tsl] if typ == "QB" else KB_T[:, tsl]
                    sc.op("act", lambda e, b=b, r=r: e.copy(out=qraw[r], in_=bank(b)), reads=[("ps", b)], writes=[("qraw", r)])
                    if ROT_DBG >= 2:
                        sc.op("dve", lambda e, b=b, r=r, tsl=tsl: e.tensor_tensor(out=t2b[r], in0=bank(b), in1=cosT[:, tsl], op=ALU.mult),
                              reads=[("ps", b), "cos"], writes=[("t2", r)])

                    def rot_tail(b=b, r=r, tsl=tsl, dst=dst, typ=typ, k=k, tb=tb):
                        pb = perm_ring.next()
                        if ROT_DBG < 3:
                            return
                        sc.op("pe", lambda e: e.matmul(bank(pb), lhsT=perm_bf, rhs=qraw[r], start=True, stop=True),
                              reads=[("qraw", r), "cbf"], writes=[("ps", pb)])
                        if ROT_DBG < 4:
                            return
                        sc.op("dve", lambda e: e.tensor_tensor(out=t3b[r], in0=bank(pb), in1=sinT[:, tsl], op=ALU.mult),
                              reads=[("ps", pb), "sin"], writes=[("t3", r)])
                        if ROT_DBG < 5:
                            return
                        sc.op(ROT_ADD_ENG, lambda e: e.tensor_tensor(out=dst, in0=t2b[r], in1=t3b[r], op=ALU.add),
                              reads=[("t2", r), ("t3", r)], writes=[(typ, k, tb)])
                    pending.append(rot_tail)
                emit_v_job()
            if c == 4:
                while pending:
                    pending.pop(0)()
                while v_next[0] < 16:
                    emit_v_job()
            dead = ["wv", "cos", "sin"] + [(nm, r) for nm in ("qraw", "t2", "t3") for r in range(2)]
            if PREFETCH and c == 6:
                sc.op("sp", lambda e: e.dma_start(out=gbt, in_=gb_d), writes=["gbt"] + dead, dma="gbt")
            if PREFETCH and c == 9:
                sc.op("pool", lambda e: e.dma_start(out=tblA.rearrange("p a b c -> p (a b c)"), in_=tbl_d),
                      writes=["tblA"] + dead, dma="tblA")
            if PREFETCH and c == 14:
                sc.op("pool", lambda e: e.dma_start(out=Wout, in_=wout_d.rearrange("(kc p) n -> p kc n", p=128)),
                      writes=["Wout"] + dead, dma="Wout")
            if stop is not None and stop.startswith("AC") and c == int(stop[2:]):
                while pending:
                    pending.pop(0)()
                dump(arena[:, 0:53248], 53248)
            if c % 2 == 1 or c == N_FM - 1:
                nb_ = c // 2 + 2
                if nb_ < len(wblocks):
                    load_wblock(nb_)
        while pending:
            pending.pop(0)()
        while v_next[0] < 16:
            emit_v_job()

        sc.barrier()
        if stop == "A":
            dump(arena[:, 0:53248], 53248)

        if not PREFETCH:
            sc.op("pool", lambda e: e.dma_start(out=tblA.rearrange("p a b c -> p (a b c)"), in_=tbl_d), writes=["tblA"], dma="tblA")
            sc.op("pool", lambda e: e.dma_start(out=Wout, in_=wout_d.rearrange("(kc p) n -> p kc n", p=128)), writes=["Wout"], dma="Wout")
            sc.op("sp", lambda e: e.dma_start(out=gbt, in_=gb_d), writes=["gbt"], dma="gbt")
        sc.op("sp", lambda e: e.dma_start(out=VAo[0:64, :, :], in_=VA[64:128, 0:15, :]), writes=["VAo"], dma="vao0")
        sc.op("sp", lambda e: e.dma_start(out=VAo[64:128, :, :], in_=VA[0:64, 1:16, :]), writes=["VAo"], dma="vao1")

        bias_rr = [0]
        if not BIAS_PE:
            tflat = tblA.rearrange("p a b c -> p (a b c)")
            for k in range(14):
                sc.op("act", lambda e, k=k: e.activation(out=tflat[:, k * 512:(k + 1) * 512], in_=tflat[:, k * 512:(k + 1) * 512], func=AF.Exp),
                      reads=["tblA"], writes=["tblA"])
        acc_ring = Ring([0, 2])
        scr_ring = Ring([4, 5, 6, 7])
        pt_ring = Ring([0, 1, 2, 3])
        nrm_ring = Ring([0, 1])
        pv_pending = []

        def flush_pv(keep=0):
            while len(pv_pending) > keep:
                pv_pending.pop(0)()

        def zero_acc(b):
            sc.op("dve", lambda e: e.memset(ps[:, b * 512:(b + 2) * 512], 0.0), writes=[("ps", b), ("ps", b + 1)])

        def na_unit(QB, hp, ms):
            ab = acc_ring.next()
            OA, DA = ab, ab + 1
            jobs = _na_jobs(QB)
            banks_jobs = []
            cur = []
            ncol = 0
            for a0, Rs in jobs:
                n = 64 * len(Rs)
                if ncol + n > 512:
                    banks_jobs.append(cur)
                    cur = []
                    ncol = 0
                cur.append((a0, Rs, ncol))
                ncol += n
            if cur:
                banks_jobs.append(cur)
            for bj in banks_jobs:
                sbs = [scr_ring.next(), scr_ring.next()]
                pts = [pt_ring.next(), pt_ring.next()]
                tot = bj[-1][2] + 64 * len(bj[-1][1])
                g_bias, g_s = [[], []], [[], []]
                for hh in range(2):
                    h = 2 * hp + hh
                    pbase = 64 * hh
                    sb = sbs[hh]
                    for a0, Rs, co in bj:
                        rc = co
                        for (R0, st, n) in _pruns(Rs):
                            o3 = ps[:, sb * 512 + rc:sb * 512 + rc + 64 * n].rearrange("p (n c) -> p n c", c=64)
                            dri0 = 6 - (a0 - R0)
                            brhs = tblA[:, h, dri0:dri0 + st * (n - 1) + 1:st, :]
                            qv = QA_T[pbase:pbase + 64, hp, :].rearrange("p (r c) -> p r c", c=64)[:, R0:R0 + st * (n - 1) + 1:st, :]
                            g_bias[hh].append((o3, brhs))
                            g_s[hh].append((o3, qv, KA_T[pbase:pbase + 64, hp, a0 * 64:a0 * 64 + 128]))
                            rc += 64 * n
                for hh in range(2):
                    for gi, (o3, brhs) in enumerate(g_bias[hh]):
                        sc.op("pe", lambda e, o3=o3, brhs=brhs, gi=gi: e.matmul(
                            o3, lhsT=ident_bf, rhs=brhs, start=(gi == 0), stop=False, skip_group_check=True),
                            reads=["tblA", "cbf"], writes=[("ps", sbs[hh])])
                for gi in range(len(g_s[0])):
                    for hh in range(2):
                        (o3, qv, kl) = g_s[hh][gi]
                        sc.op("pe", lambda e, o3=o3, qv=qv, kl=kl, tp_=(64 * hh, 0): e.matmul(
                            o3, lhsT=kl, rhs=qv, start=False, stop=True, tile_position=tp_, skip_group_check=True),
                            reads=[], writes=[("ps", sbs[hh])])
                flush_pv(2)
                for hh in range(2):
                    sc.op("act", lambda e, sb=sbs[hh], pt=pts[hh], tot=tot: e.activation(out=PT[pt][:, 0:tot], in_=bank(sb, tot), func=AF.Exp),
                          reads=[("ps", sbs[hh])], writes=[("PT", pts[hh])])
                for hh in range(2):
                    def pv(bj=bj, pt=pts[hh], h=2 * hp + hh, pbase=64 * hh, OA=OA, DA=DA):
                        for a0, Rs, co in bj:
                            rc = co
                            vl = VA[:, a0 // 2, h * 64:(h + 1) * 64] if a0 % 2 == 0 else VAo[:, (a0 - 1) // 2, h * 64:(h + 1) * 64]
                            for (R0, st, n) in _pruns(Rs):
                                Rl = R0 - 8 * QB
                                pos0 = (R0 % 2) * 4 + Rl // 2
                                oa = ps[pbase:pbase + 64, OA * 512 + pos0 * 64:OA * 512 + (pos0 + n) * 64]
                                da = ps[pbase:pbase + 64, DA * 512 + pos0 * 64:DA * 512 + (pos0 + n) * 64]
                                p3 = PT[pt][:, rc:rc + 64 * n]
                                for (o_, l_, bk) in ((oa, vl, OA), (da, ones_bf[:, 0:64], DA)):
                                    sc.op("pe", lambda e, o_=o_, l_=l_, p3=p3, tp_=(0, pbase): e.matmul(
                                        o_, lhsT=l_, rhs=p3, start=False, stop=False, skip_group_check=True, tile_position=tp_),
                                        reads=[("PT", pt), "ones", "VAo"], writes=[("ps", bk)])
                                rc += 64 * n
                    pv_pending.append(pv)
            flush_pv()
            nr = nrm_ring.next()
            qsl = slice(QB * 512, (QB + 1) * 512)
            if NA_RECIP_DVE:
                sc.op("dve", lambda e: e.reciprocal(out=rdenb[nr], in_=bank(DA)), reads=[("ps", DA)], writes=[("rden", nr)])
            else:
                sc.op("act", lambda e: e.activation(out=rdenb[nr], in_=bank(DA), func=AF.Ln), reads=[("ps", DA)], writes=[("rden", nr)])
                sc.op("act", lambda e: e.activation(out=rdenb[nr], in_=rdenb[nr], func=AF.Exp, scale=-1.0),
                      reads=[("rden", nr)], writes=[("rden", nr)])
            sc.op("dve", lambda e: e.tensor_tensor(out=g2b[nr].rearrange("p (i par c) -> p par i c", par=2, c=64),
                                                   in0=bank(OA).rearrange("p (par i c) -> p par i c", par=2, c=64),
                                                   in1=rdenb[nr].rearrange("p (par i c) -> p par i c", par=2, c=64), op=ALU.mult),
                  reads=[("ps", OA), ("rden", nr)], writes=[("g2", nr)])
            sc.op("pool", lambda e: e.tensor_tensor(out=mixT[ms][:, hp, :], in0=g2b[nr], in1=gateA[:, hp, qsl], op=ALU.mult),
                  reads=[("g2", nr)], writes=[("mix", ms, hp)])
            zero_acc(ab)

        def swa_unit(n, ms):
            ab = acc_ring.next()
            OB, DB = ab, ab + 1
            nl = n % 4
            for kb in (n - 1, n, n + 1):
                if kb < 0 or kb > 15:
                    continue
                sbs = [scr_ring.next(), scr_ring.next()]
                pts = [pt_ring.next(), pt_ring.next()]
                if kb != n:
                    kind = 0 if kb < n else 1
                    for g in range(2):
                        sc.op("pe", lambda e, sb=sbs[g], kind=kind: e.matmul(bank(sb), lhsT=ident_bf, rhs=masks_bf[:, kind, :], start=True, stop=False),
                              reads=["cbf"], writes=[("ps", sbs[g])])
                for j in range(4):
                    for g in range(2):
                        pb = 64 * g
                        st_ = (kb == n)
                        sp_ = (kb == n) or (j == 3)
                        sc.op("pe", lambda e, sb=sbs[g], j=j, kb=kb, pb=pb, st_=st_, sp_=sp_: e.matmul(
                            ps[:, sb * 512 + j * 128:sb * 512 + (j + 1) * 128], lhsT=KB_T[pb:pb + 64, kb * 128:(kb + 1) * 128],
                            rhs=QB_T[pb:pb + 64, j, n * 128:(n + 1) * 128], start=st_, stop=sp_, tile_position=(pb, 0)),
                            reads=[], writes=[("ps", sbs[g])])
                flush_pv(2)
                for g in range(2):
                    sc.op("act", lambda e, sb=sbs[g], pt=pts[g]: e.activation(out=PT[pt], in_=bank(sb), func=AF.Exp, scale=0.125),
                          reads=[("ps", sbs[g])], writes=[("PT", pts[g])])
                for g in range(2):
                    def pv(pt=pts[g], kb=kb, pb=64 * g, OB=OB, DB=DB):
                        sc.op("pe", lambda e: e.matmul(ps[pb:pb + 64, OB * 512:OB * 512 + 512], lhsT=VB[:, kb, pb:pb + 64], rhs=PT[pt],
                                                       start=False, stop=False, skip_group_check=True, tile_position=(0, pb)),
                              reads=[("PT", pt)], writes=[("ps", OB)])
                        sc.op("pe", lambda e: e.matmul(ps[pb:pb + 64, DB * 512:DB * 512 + 512], lhsT=ones_bf[:, 0:64], rhs=PT[pt],
                                                       start=False, stop=False, skip_group_check=True, tile_position=(0, pb)),
                              reads=[("PT", pt)], writes=[("ps", DB)])
                    pv_pending.append(pv)
            flush_pv()
            nr = nrm_ring.next()
            d3 = rdenb[nr].rearrange("p (j q) -> p j q", j=4)
            sc.op("dve", lambda e: e.tensor_tensor(out=d3, in0=bank(DB).rearrange("p (j q) -> p j q", j=4),
                                                   in1=sinkE.unsqueeze(2).broadcast_to([128, 4, 128]), op=ALU.add),
                  reads=[("ps", DB), "sinkE"], writes=[("rden", nr)])
            if n % 4 < 2:
                sc.op("dve", lambda e: e.reciprocal(out=rdenb[nr], in_=rdenb[nr]), reads=[("rden", nr)], writes=[("rden", nr)])
            else:
                sc.op("act", lambda e: e.activation(out=rdenb[nr], in_=rdenb[nr], func=AF.Ln), reads=[("rden", nr)], writes=[("rden", nr)])
                sc.op("act", lambda e: e.activation(out=rdenb[nr], in_=rdenb[nr], func=AF.Exp, scale=-1.0),
                      reads=[("rden", nr)], writes=[("rden", nr)])
            sc.op("dve", lambda e: e.tensor_tensor(out=g2b[nr], in0=bank(OB), in1=rdenb[nr], op=ALU.mult),
                  reads=[("ps", OB), ("rden", nr)], writes=[("g2", nr)])
            sc.op("pool", lambda e: e.tensor_tensor(out=mixT[ms][:, 4:8, nl * 128:(nl + 1) * 128],
                                                    in0=g2b[nr].rearrange("p (j q) -> p j q", j=4),
                                                    in1=gateB[:, :, n * 128:(n + 1) * 128], op=ALU.mult),
                  reads=[("g2", nr)], writes=[("mix", ms, 4 + nl)])
            zero_acc(ab)

        def load_xres(i):
            sl = i % 4
            sc.op("sp", lambda e: e.dma_start(out=xf[sl], in_=x_d[i * 128:(i + 1) * 128, :]), writes=[("xf", sl)], dma=("xf", sl))

        out_back = []

        def out_unit(i, ms):
            il = i % 4
            sl = i % 4
            b = scr_ring.next_pair()
            for nb in range(2):
                for c in range(8):
                    sc.op("pe", lambda e, nb=nb, c=c: e.matmul(bank(b + nb), lhsT=mixT[ms][:, c, il * 128:(il + 1) * 128],
                                                              rhs=Wout[:, c, nb * 512:(nb + 1) * 512], start=(c == 0), stop=(c == 7)),
                          reads=[("mix", ms, cc) for cc in range(4)] + [("mix", ms, 4 + il), "Wout"],
                          writes=[("ps", b + nb)])
            yps = ps[:, b * 512:b * 512 + 1024]
            sc.op("dve", lambda e: e.scalar_tensor_tensor(out=xf[sl], in0=xf[sl], scalar=ALPHA, in1=yps, op0=ALU.mult, op1=ALU.add),
                  reads=[("ps", b), ("ps", b + 1), ("xf", sl)], writes=[("xf", sl)])
            sc.op("dve", lambda e: e.bn_stats(out=stt[sl][:, 0:6], in_=xf[sl][:, 0:512]), reads=[("xf", sl)], writes=[("st", sl, 0)])
            sc.op("dve", lambda e: e.bn_stats(out=stt[sl][:, 6:12], in_=xf[sl][:, 512:1024]), reads=[("xf", sl)], writes=[("st", sl, 1)])
            sc.op("dve", lambda e: e.bn_aggr(out=mvt[sl], in_=stt[sl]), reads=[("st", sl, 0), ("st", sl, 1)], writes=[("mv", sl)])
            sc.op("dve", lambda e: e.tensor_scalar_add(out=rst[sl][:, 1:2], in0=mvt[sl][:, 1:2], scalar1=LN_EPS),
                  reads=[("mv", sl)], writes=[("rs", sl, 1)])
            sc.op("act", lambda e: e.activation(out=rst[sl][:, 1:2], in_=rst[sl][:, 1:2], func=AF.Ln),
                  reads=[("rs", sl, 1)], writes=[("rs", sl, 1)])
            sc.op("act", lambda e: e.activation(out=rst[sl][:, 0:1], in_=rst[sl][:, 1:2], func=AF.Exp, scale=-0.5),
                  reads=[("rs", sl, 1)], writes=[("rs", sl, 0)])

            def back():
                if i >= 14:
                    sc.op("dve", lambda e: e.scalar_tensor_tensor(out=xf[sl], in0=xf[sl], scalar=mvt[sl][:, 0:1], in1=gbt[:, 0:D],
                                                                  op0=ALU.subtract, op1=ALU.mult),
                          reads=[("xf", sl), ("mv", sl), "gbt"], writes=[("xf", sl)])
                    sc.op("dve", lambda e: e.scalar_tensor_tensor(out=xf[sl], in0=xf[sl], scalar=rst[sl][:, 0:1], in1=gbt[:, D:2 * D],
                                                                  op0=ALU.mult, op1=ALU.add),
                          reads=[("xf", sl), ("rs", sl, 0), "gbt"], writes=[("xf", sl)])
                else:
                    sc.op("dve", lambda e: e.tensor_scalar(out=xf[sl], in0=xf[sl], scalar1=mvt[sl][:, 0:1], scalar2=rst[sl][:, 0:1],
                                                           op0=ALU.subtract, op1=ALU.mult),
                          reads=[("xf", sl), ("mv", sl), ("rs", sl, 0)], writes=[("xf", sl)])
                    sc.op("dve" if i >= 12 else "pool", lambda e: e.tensor_tensor(out=xf[sl], in0=xf[sl], in1=gbt[:, 0:D], op=ALU.mult),
                          reads=[("xf", sl), "gbt"], writes=[("xf", sl)])
                    sc.op("pool", lambda e: e.tensor_tensor(out=xf[sl], in0=xf[sl], in1=gbt[:, D:2 * D], op=ALU.add),
                          reads=[("xf", sl), "gbt"], writes=[("xf", sl)])
                sc.op("sp", lambda e: e.dma_start(out=out_d[i * 128:(i + 1) * 128, :], in_=xf[sl]),
                      reads=[("xf", sl)], dma=("xo", sl))
            while out_back:
                out_back.pop(0)()
            out_back.append(back)

        zero_acc(0)
        zero_acc(2)
        for QB in range(4):
            ms = QB % 2
            if QB == 0:
                for n in range(4):
                    swa_unit(n, ms)
                for i in range(4):
                    load_xres(i)
                for hp in range(4):
                    na_unit(QB, hp, ms)
                if stop == "B0":
                    dump(mixT[0].rearrange("p a b -> p (a b)"), 4096)
                for i in range(4):
                    out_unit(i, ms)
            else:
                for i in range(4 * QB, 4 * QB + 4):
                    load_xres(i)
                for k in range(4):
                    na_unit(QB, k, ms)
                    swa_unit(4 * QB + k, ms)
                for n in range(4 * QB, 4 * QB + 4):
                    out_unit(n, ms)
            while out_back:
                out_back.pop(0)()
            if stop is not None and stop.startswith("CQ") and QB == int(stop[2:]):
                dump(mixT[0].rearrange("p a b -> p (a b)"), 4096)

    try:
        body()
    except _Stop:
        pass

    fin = [(sk, v) for sk, v in sc.dma_cnt.items() if isinstance(sk[1], tuple) and sk[1][0] == "xo"]
    sc.streams["sp"].append((fin, None, None))

    with ExitStack() as es:
        sems = {}
        for e in ENGS:
            sems[e] = es.enter_context(nc.semaphore("s_" + e))
        for k, sk in enumerate(sc.dma_cnt.keys()):
            sems[sk] = es.enter_context(nc.semaphore("d_%d" % k))
        block = es.enter_context(nc.Block())

        def emit(eng_handle, name):
            for waits, fn, inc in sc.streams[name]:
                for sk, v in waits:
                    eng_handle.wait_ge(sems[sk], v)
                if fn is not None:
                    ins = fn(eng_handle)
                    ins.then_inc(sems[inc[0]], inc[1])

        @block.tensor
        def _(t):
            emit(t, "pe")

        @block.scalar
        def _(a):
            emit(a, "act")

        @block.vector
        def _(v):
            emit(v, "dve")

        @block.gpsimd
        def _(g):
            emit(g, "pool")

        @block.sync
        def _(s):
            emit(s, "sp")
    return nc


_NC_CACHE = {}


def kernel(x, w_in, rel_pos_bias, sink_logits, w_out, ln_gain, ln_bias):
    x = np.asarray(x, dtype=np.float32)
    w_in = np.asarray(w_in, dtype=np.float32)[0]
    w_out = np.asarray(w_out, dtype=np.float32)[0]
    rpb = np.asarray(rel_pos_bias, dtype=np.float32)[0]
    sink = np.asarray(sink_logits, dtype=np.float32)[0]
    gain = np.asarray(ln_gain, dtype=np.float32)[0]
    bias = np.asarray(ln_bias, dtype=np.float32)[0]

    w_in_p = np.ascontiguousarray(w_in[:, _w_in_perm()])
    w_out_p = np.ascontiguousarray(w_out[_w_out_perm(), :])
    tbl_a = _na_table(rpb)
    cmat, cosT, sinT = _consts()
    sink_lay = np.ascontiguousarray(np.repeat(sink.reshape(2, 4), 64, axis=0))
    gb = np.ascontiguousarray(np.tile(np.concatenate([gain, bias])[None, :], (128, 1)))

    if "nc" not in _NC_CACHE:
        _NC_CACHE["nc"] = build_nc()
    nc = _NC_CACHE["nc"]
    shared = {"w_in_p": w_in_p, "w_out_p": w_out_p, "tbl_a": tbl_a, "cmat": cmat, "cos_t": cosT, "sin_t": sinT,
              "sink_lay": sink_lay, "gain_bias": gb}
    in_maps = []
    for b in range(N_CORES):
        m = dict(shared)
        m["x"] = np.ascontiguousarray(x[b])
        in_maps.append(m)
    res = run_bass_kernel_spmd(nc, in_maps, core_ids=list(range(N_CORES)))
    out = np.stack([np.asarray(r["out"], dtype=np.float32).reshape(S, D) for r in res.results], axis=0)
    return out
```

```python
import numpy as np
from contextlib import ExitStack
import concourse.bass as bass
import concourse.mybir as mybir
from concourse.bass_utils import run_bass_kernel_spmd

F32 = mybir.dt.float32
BF = mybir.dt.bfloat16
AF = mybir.ActivationFunctionType
ALU = mybir.AluOpType

S = 2048
D = 1024
NEG = -30000.0
ALPHA = float(2.0 ** 0.25)
LN_EPS = 1e-5
N_CORES = 8
import os
ROT_ADD_ENG = os.environ.get('ROT_ADD_ENG', 'pool')
SAME_ENG_SYNC = int(os.environ.get('SAME_ENG_SYNC', '1'))
GROUP_S = int(os.environ.get('GROUP_S', '1'))
GROUP_PV = int(os.environ.get('GROUP_PV', '0'))
LOOKAHEAD = int(os.environ.get('LOOKAHEAD', '3'))
BIAS_PE = int(os.environ.get('BIAS_PE', '1'))
PREFETCH = int(os.environ.get('PREFETCH', '0'))
NA_RECIP_DVE = int(os.environ.get('NA_RECIP_DVE', '1'))
ROT_DBG = int(os.environ.get('ROT_DBG', '5'))

FM_TYPES = [("KA", k) for k in range(4)] + [("QA", k) for k in range(4)] + [("QB", k) for k in range(4)] + \
           [("KB", 0)] + [("ZA", k) for k in range(4)] + [("ZB", k) for k in range(4)]
N_FM = len(FM_TYPES)
FM_COLS = N_FM * 128
V_COLS = 640


def _w_in_perm():
    cols = []
    qa, ka, va, za, qb, kb, vb, zb = 0, 512, 1024, 1536, 2048, 2560, 2688, 2816
    for t, k in FM_TYPES:
        if t == "KA":
            cols += list(range(ka + 128 * k, ka + 128 * (k + 1)))
        elif t == "QA":
            cols += list(range(qa + 128 * k, qa + 128 * (k + 1)))
        elif t == "ZA":
            cols += list(range(za + 128 * k, za + 128 * (k + 1)))
        elif t == "QB":
            cols += list(range(qb + 64 * k, qb + 64 * (k + 1))) + list(range(qb + 64 * (4 + k), qb + 64 * (5 + k)))
        elif t == "ZB":
            cols += list(range(zb + 64 * k, zb + 64 * (k + 1))) + list(range(zb + 64 * (4 + k), zb + 64 * (5 + k)))
        elif t == "KB":
            cols += list(range(kb, kb + 128))
    cols += list(range(va, va + 512)) + list(range(vb, vb + 128))
    return np.array(cols, dtype=np.int64)


def _w_out_perm():
    rows = list(range(512))
    for j in range(4):
        rows += list(range(512 + 64 * j, 512 + 64 * (j + 1))) + list(range(512 + 64 * (4 + j), 512 + 64 * (5 + j)))
    return np.array(rows, dtype=np.int64)


def _na_table(rpb):
    jr = np.arange(128) // 64
    jc = np.arange(128) % 64
    qc = np.arange(64)
    cs = np.clip(qc - 8, 0, 48)
    colvalid = (jc[:, None] >= cs[None, :]) & (jc[:, None] < cs[None, :] + 16)
    dc_idx = np.clip(jc[:, None] - qc[None, :] + 15, 0, 30)
    tbl = np.full((128, 8, 14, 64), NEG, dtype=np.float32)
    for dri in range(14):
        dr0 = 6 - dri
        dr_idx = dr0 + jr + 7
        g = rpb[:, dr_idx[:, None], dc_idx]
        g = np.transpose(g, (1, 0, 2))
        tbl[:, :, dri, :] = np.where(colvalid[:, None, :], g, np.float32(NEG))
    return np.ascontiguousarray(tbl.reshape(128, 8 * 14 * 64))


def _consts():
    ident = np.eye(128, dtype=np.float32)
    swap = np.zeros((128, 128), np.float32)
    for m in range(128):
        swap[(m + 64) % 128, m] = 1.0
    perm = np.zeros((128, 128), np.float32)
    for p in range(128):
        if p % 64 < 32:
            perm[p + 32, p] = -1.0
        else:
            perm[p - 32, p] = 1.0
    j = np.arange(128)[:, None]
    q = np.arange(128)[None, :]
    mprev = np.where(j >= q, 0.0, NEG).astype(np.float32)
    mnext = np.where(j <= q, 0.0, NEG).astype(np.float32)
    masks = np.concatenate([np.tile(mprev, (1, 4)), np.tile(mnext, (1, 4))], axis=1)
    cmat = np.concatenate([ident, swap, perm, masks], axis=1)
    inv_freq = (np.float32(10000.0) ** (-(np.arange(0, 64, 2, dtype=np.float32)) / np.float32(64))).astype(np.float32)
    ang = (np.arange(S, dtype=np.float32)[:, None] * inv_freq[None, :]).astype(np.float32)
    cosT = np.cos(ang).astype(np.float32).T
    sinT = np.sin(ang).astype(np.float32).T
    cosT = np.ascontiguousarray(np.tile(cosT, (4, 1)))
    sinT = np.ascontiguousarray(np.tile(sinT, (4, 1)))
    return np.ascontiguousarray(cmat), cosT, sinT


ENGS = ("pe", "act", "dve", "pool", "sp")


class Sched:
    def __init__(self):
        self.streams = {e: [] for e in ENGS}
        self.cnt = {e: 0 for e in ENGS}
        self.dma_cnt = {}
        self.lastw = {}
        self.readers = {}
        self.waited = {e: {} for e in ENGS}

    def _deps(self, reads, writes):
        deps = {}

        def add(ev):
            if ev is None:
                return
            sk, v = ev
            if deps.get(sk, 0) < v:
                deps[sk] = v
        for r in reads:
            add(self.lastw.get(r))
        for w in writes:
            add(self.lastw.get(w))
            for sk, v in self.readers.get(w, {}).items():
                add((sk, v))
        return deps

    def op(self, eng, fn, reads=(), writes=(), dma=None):
        writes = list(writes) + [r for r in reads if isinstance(r, tuple) and r[0] == "ps"]
        if eng == "pool" and dma is not None:
            writes.append("poolq")
        reads = [r for r in reads if not (isinstance(r, tuple) and r[0] == "ps")]
        deps = self._deps(reads, writes)
        waits = []
        for sk, v in deps.items():
            if sk == eng and (eng == "pe" or not SAME_ENG_SYNC):
                continue
            if self.waited[eng].get(sk, 0) >= v:
                continue
            self.waited[eng][sk] = v
            waits.append((sk, v))
        if dma is None:
            self.cnt[eng] += 1
            ev = (eng, self.cnt[eng])
            inc = (eng, 1)
        else:
            sk = ("dma", dma)
            self.dma_cnt[sk] = self.dma_cnt.get(sk, 0) + 16
            ev = (sk, self.dma_cnt[sk])
            inc = (sk, 16)
        self.streams[eng].append((waits, fn, inc))
        for r in reads:
            d = self.readers.setdefault(r, {})
            if d.get(ev[0], 0) < ev[1]:
                d[ev[0]] = ev[1]
        for w in writes:
            self.lastw[w] = ev
            self.readers[w] = {}
        return ev

    def barrier(self):
        cur = {e: self.cnt[e] for e in ENGS if self.cnt[e] > 0}
        cur.update(self.dma_cnt)
        for e in ENGS:
            waits = []
            for sk, v in cur.items():
                if sk == e:
                    continue
                if self.waited[e].get(sk, 0) >= v:
                    continue
                self.waited[e][sk] = v
                waits.append((sk, v))
            if waits:
                self.streams[e].append((waits, None, None))


class Ring:
    def __init__(self, items):
        self.items = list(items)
        self.i = 0

    def next(self):
        v = self.items[self.i % len(self.items)]
        self.i += 1
        return v

    def next_pair(self):
        if self.i % 2 == 1:
            self.i += 1
        a = self.items[self.i % len(self.items)]
        self.i += 2
        return a


def _runs(Rs):
    out = []
    i = 0
    while i < len(Rs):
        if i + 1 < len(Rs):
            st = Rs[i + 1] - Rs[i]
            j = i + 1
            while j + 1 < len(Rs) and Rs[j + 1] - Rs[j] == st:
                j += 1
            out.append((Rs[i], st, j - i + 1))
            i = j + 1
        else:
            out.append((Rs[i], 1, 1))
            i += 1
    return out


def _pruns(Rs):
    out = []
    for par in (0, 1):
        sub = [R for R in Rs if R % 2 == par]
        i = 0
        while i < len(sub):
            j = i
            while j + 1 < len(sub) and sub[j + 1] - sub[j] == 2:
                j += 1
            out.append((sub[i], 2, j - i + 1))
            i = j + 1
    return out


def _na_jobs(QB):
    by = {}
    for R in range(8 * QB, 8 * QB + 8):
        r0 = min(max(R - 4, 0), 24)
        for a0 in (r0, r0 + 2, r0 + 4, r0 + 6):
            by.setdefault(a0, []).append(R)
    return sorted(by.items())


def build_nc(stop=None):
    nc = bass.Bass("TRN2", target_bir_lowering=False)
    x_d = nc.dram_tensor("x", [S, D], F32, kind="ExternalInput").ap()
    win_d = nc.dram_tensor("w_in_p", [D, FM_COLS + V_COLS], F32, kind="ExternalInput").ap()
    wout_d = nc.dram_tensor("w_out_p", [D, D], F32, kind="ExternalInput").ap()
    tbl_d = nc.dram_tensor("tbl_a", [128, 8 * 14 * 64], F32, kind="ExternalInput").ap()
    cmat_d = nc.dram_tensor("cmat", [128, 1408], F32, kind="ExternalInput").ap()
    cos_d = nc.dram_tensor("cos_t", [128, S], F32, kind="ExternalInput").ap()
    sin_d = nc.dram_tensor("sin_t", [128, S], F32, kind="ExternalInput").ap()
    sink_d = nc.dram_tensor("sink_lay", [128, 4], F32, kind="ExternalInput").ap()
    gb_d = nc.dram_tensor("gain_bias", [128, 2 * D], F32, kind="ExternalInput").ap()
    out_d = nc.dram_tensor("out", [S, D], F32, kind="ExternalOutput").ap()
    dbg_d = None
    if stop is not None:
        dbg_d = nc.dram_tensor("dbg", [128, 53248], BF, kind="ExternalOutput").ap()

    ARENA_BYTES = 211456
    arena = nc.alloc_sbuf_tensor("arena", [128, ARENA_BYTES // 2], BF)
    ps = nc.alloc_psum_tensor("ps", [128, 4096], F32)

    class Carver:
        def __init__(self, base=0):
            self.off = base
            self.hi = base

        def get(self, fshape, dt):
            n = int(np.prod(fshape))
            nb = n * (4 if dt == F32 else 2)
            nb_al = (nb + 63) // 64 * 64
            off = self.off
            self.off += nb_al
            self.hi = max(self.hi, self.off)
            assert self.off <= ARENA_BYTES, f"SBUF arena overflow {self.off}"
            a = arena[:, off // 2:(off + nb) // 2]
            if dt == F32:
                a = a.bitcast(F32)
            if len(fshape) == 2:
                a = a.rearrange("p (a b) -> p a b", b=fshape[1])
            elif len(fshape) == 3:
                a = a.rearrange("p (a b c) -> p a b c", b=fshape[1], c=fshape[2])
            return a

    cv = Carver()
    QA_T = cv.get([4, S], BF)
    KA_T = cv.get([4, S], BF)
    gateA = cv.get([4, S], BF)
    gateB = cv.get([4, S], BF)
    QB_T = cv.get([4, S], BF)
    KB_T = cv.get([S], BF)
    VA = cv.get([16, 512], BF)
    VB = cv.get([16, 128], BF)
    cbf = cv.get([1408], BF)
    ident_f = cv.get([128], F32)
    ones_bf = cv.get([128], BF)
    sinkE = cv.get([4], F32)
    sink_raw = cv.get([4], F32)
    xf = [cv.get([D], F32) for _ in range(4)]
    ident_bf = cbf[:, 0:128]
    swap_bf = cbf[:, 128:256]
    perm_bf = cbf[:, 256:384]
    masks_bf = cbf[:, 384:1408].rearrange("p (k n) -> p k n", k=2)
    phase_base = cv.off
    ca = Carver(phase_base)
    xT = ca.get([8, S], BF)
    wb = [ca.get([8, 256], BF) for _ in range(3)]
    off_wv = ca.off
    wv = ca.get([8, V_COLS], BF)
    cosT = ca.get([S], F32)
    sinT = ca.get([S], F32)
    qraw = [ca.get([512], BF) for _ in range(2)]
    t2b = [ca.get([512], F32) for _ in range(2)]
    t3b = [ca.get([512], F32) for _ in range(2)]
    cb = Carver(phase_base)
    mixT = [cb.get([8, 512], BF) for _ in range(2)]
    PT = [cb.get([512], BF) for _ in range(4)]
    rdenb = [cb.get([512], F32) for _ in range(2)]
    g2b = [cb.get([512], F32) for _ in range(2)]
    stt = [cb.get([12], F32) for _ in range(4)]
    mvt = [cb.get([2], F32) for _ in range(4)]
    rst = [cb.get([2], F32) for _ in range(4)]
    VAo = cb.get([15, 512], BF)
    assert cb.off <= off_wv, (cb.off, off_wv)
    cb.off = off_wv
    Wout = cb.get([8, D], BF)
    tblA = cb.get([8, 14, 64], BF)
    gbt = cb.get([2 * D], F32)

    def bank(b, w=512):
        return ps[:, b * 512:b * 512 + w]

    sc = Sched()

    class _Stop(Exception):
        pass

    def dump(ap_src, ncols):
        sc.barrier()
        sc.op("sp", lambda e: e.dma_start(out=dbg_d[:, 0:ncols], in_=ap_src), dma=("xo", 0))
        raise _Stop()

    def body():

        sc.op("sp", lambda e: e.dma_start(out=ident_f, in_=cmat_d[:, 0:128]), writes=["ident_f"], dma="identf")
        sc.op("sp", lambda e: e.dma_start(out=sink_raw, in_=sink_d), writes=["sink_raw"], dma="sink")
        sc.op("pool", lambda e: e.memset(ones_bf, 1.0), writes=["ones"])
        sc.op("act", lambda e: e.activation(out=sinkE, in_=sink_raw, func=AF.Exp), reads=["sink_raw"], writes=["sinkE"])

        if stop == "setup":
            dump(cbf, 1408)
        def load_x(i):
            sl = i % 4
            sc.op("sp", lambda e, i=i, sl=sl: e.dma_start(out=xf[sl], in_=x_d[i * 128:(i + 1) * 128, :]),
                  writes=[("xf", sl)], dma=("xf", sl))
        for i in range(4):
            load_x(i)

        wblocks = []
        c0 = 0
        while c0 < N_FM:
            n = min(2, N_FM - c0)
            wblocks.append((c0, n))
            c0 += n
        wslot_of_chunk = {}

        def load_wblock(bi, after=()):
            c0, n = wblocks[bi]
            sl = bi % 3
            src = win_d[:, c0 * 128:(c0 + n) * 128].rearrange("(kc p) n -> p kc n", p=128)
            sc.op("pool", lambda e, sl=sl, n=n, src=src: e.dma_start(out=wb[sl][:, :, 0:n * 128], in_=src),
                  reads=list(after), writes=[("wb", sl)], dma=("wb", sl))
            for k in range(n):
                wslot_of_chunk[c0 + k] = (sl, k)

        sc.op("pool", lambda e: e.dma_start(out=wv, in_=win_d[:, FM_COLS:FM_COLS + V_COLS].rearrange("(kc p) n -> p kc n", p=128)),
              writes=["wv"], dma="wv")
        sc.op("pool", lambda e: e.dma_start(out=cbf, in_=cmat_d), writes=["cbf"], dma="cbf")
        if stop == "A0":
            dump(wv.rearrange("p a b -> p (a b)"), 8 * V_COLS)

        fm_ring = Ring([0, 1, 2, 3])
        perm_ring = Ring([4, 5])
        rot_i = [0]
        pending = []
        v_next = [0]

        def emit_v_job():
            i = v_next[0]
            if i >= 16:
                return
            v_next[0] += 1
            for kc in range(8):
                sc.op("pe", lambda e, kc=kc, i=i: e.matmul(bank(6), lhsT=xT[:, kc, i * 128:(i + 1) * 128], rhs=wv[:, kc, 0:512],
                                                            start=(kc == 0), stop=(kc == 7)),
                      reads=[("xT", i), "wv"], writes=[("ps", 6)])
            for kc in range(8):
                sc.op("pe", lambda e, kc=kc, i=i: e.matmul(bank(7, 128), lhsT=xT[:, kc, i * 128:(i + 1) * 128], rhs=wv[:, kc, 512:640],
                                                            start=(kc == 0), stop=(kc == 7)),
                      reads=[("xT", i), "wv"], writes=[("ps", 7)])
            sc.op("dve", lambda e, i=i: e.tensor_copy(out=VA[:, i, :], in_=bank(6)), reads=[("ps", 6)], writes=[("VA", i)])
            sc.op("act", lambda e, i=i: e.copy(out=VB[:, i, :], in_=bank(7, 128)), reads=[("ps", 7)], writes=[("VB", i)])

        for i in range(16):
            sl = i % 4
            b = 2 * (i % 2)
            for kc in range(8):
                sc.op("pe", lambda e, b=b, kc=kc, sl=sl: e.transpose(out=ps[:, b * 512 + kc * 128:b * 512 + (kc + 1) * 128],
                                                                      in_=xf[sl][:, kc * 128:(kc + 1) * 128], identity=ident_f),
                      reads=[("xf", sl), "ident_f"], writes=[("ps", b), ("ps", b + 1)])
            src = ps[:, b * 512:b * 512 + 1024].rearrange("p (k t) -> p k t", k=8)
            dst = xT[:, :, i * 128:(i + 1) * 128]
            if i % 2 == 0:
                sc.op("dve", lambda e, src=src, dst=dst: e.tensor_copy(out=dst, in_=src),
                      reads=[("ps", b), ("ps", b + 1)], writes=[("xT", i)])
            else:
                sc.op("act", lambda e, src=src, dst=dst: e.copy(out=dst, in_=src),
                      reads=[("ps", b), ("ps", b + 1)], writes=[("xT", i)])
            if i + 4 < 16:
                load_x(i + 4)
            if i == 11:
                sc.op("sp", lambda e: e.dma_start(out=cosT, in_=cos_d), writes=["cos"], dma="cos")
                sc.op("sp", lambda e: e.dma_start(out=sinT, in_=sin_d), writes=["sin"], dma="sin")
            if i == 11:
                load_wblock(0, after=[("xT", 11)])
                load_wblock(1, after=[("xT", 11)])
            if i >= 1:
                emit_v_job()
        if stop == "xT":
            dump(xT.rearrange("p a b -> p (a b)"), 16384)

        for c in range(N_FM):
            typ, k = FM_TYPES[c]
            bi = c // 2
            if c % 2 == 0 and bi + 2 < len(wblocks):
                pass
            sl, kk = wslot_of_chunk[c]
            for tb in range(4):
                b = fm_ring.next()
                tsl = slice(tb * 512, (tb + 1) * 512)
                for kc in range(8):
                    sc.op("pe", lambda e, b=b, kc=kc, sl=sl, kk=kk, tsl=tsl: e.matmul(
                        bank(b), lhsT=wb[sl][:, kc, kk * 128:(kk + 1) * 128], rhs=xT[:, kc, tsl], start=(kc == 0), stop=(kc == 7)),
                        reads=[("xT", 4 * tb), ("xT", 4 * tb + 1), ("xT", 4 * tb + 2), ("xT", 4 * tb + 3), ("wb", sl)], writes=[("ps", b)])
                while pending:
                    pending.pop(0)()
                if typ == "KA":
                    sc.op("dve", lambda e, b=b, k=k, tsl=tsl: e.tensor_copy(out=KA_T[:, k, tsl], in_=bank(b)),
                          reads=[("ps", b)], writes=[("KA", k, tb)])
                elif typ == "QA":
                    sc.op("act", lambda e, b=b, k=k, tsl=tsl: e.mul(out=QA_T[:, k, tsl], in_=bank(b), mul=0.125),
                          reads=[("ps", b)], writes=[("QA", k, tb)])
                elif typ in ("ZA", "ZB"):
                    dst = (gateA if typ == "ZA" else gateB)[:, k, tsl]
                    sc.op("act", lambda e, b=b, dst=dst: e.activation(out=dst, in_=bank(b), func=AF.Silu),
                          reads=[("ps", b)], writes=[(typ, k, tb)])
                else:
                    r = rot_i[0] % 2
                    rot_i[0] += 1
                    dst = QB_T[:, k, tsl] if typ == "QB" else KB_T[:, tsl]
                    sc.op("act", lambda e, b=b, r=r: e.copy(out=qraw[r], in_=bank(b)), reads=[("ps", b)], writes=[("qraw", r)])
                    if ROT_DBG >= 2:
                        sc.op("dve", lambda e, b=b, r=r, tsl=tsl: e.tensor_tensor(out=t2b[r], in0=bank(b), in1=cosT[:, tsl], op=ALU.mult),
                              reads=[("ps", b), "cos"], writes=[("t2", r)])

                    def rot_tail(b=b, r=r, tsl=tsl, dst=dst, typ=typ, k=k, tb=tb):
                        pb = perm_ring.next()
                        if ROT_DBG < 3:
                            return
                        sc.op("pe", lambda e: e.matmul(bank(pb), lhsT=perm_bf, rhs=qraw[r], start=True, stop=True),
                              reads=[("qraw", r), "cbf"], writes=[("ps", pb)])
                        if ROT_DBG < 4:
                            return
                        sc.op("dve", lambda e: e.tensor_tensor(out=t3b[r], in0=bank(pb), in1=sinT[:, tsl], op=ALU.mult),
                              reads=[("ps", pb), "sin"], writes=[("t3", r)])
                        if ROT_DBG < 5:
                            return
                        sc.op(ROT_ADD_ENG, lambda e: e.tensor_tensor(out=dst, in0=t2b[r], in1=t3b[r], op=ALU.add),
                              reads=[("t2", r), ("t3", r)], writes=[(typ, k, tb)])
                    pending.append(rot_tail)
                emit_v_job()
            if c == 4:
                while pending:
                    pending.pop(0)()
                while v_next[0] < 16:
                    emit_v_job()
            dead = ["wv", "cos", "sin"] + [(nm, r) for nm in ("qraw", "t2", "t3") for r in range(2)]
            if PREFETCH and c == 6:
                sc.op("sp", lambda e: e.dma_start(out=gbt, in_=gb_d), writes=["gbt"] + dead, dma="gbt")
            if PREFETCH and c == 9:
                sc.op("pool", lambda e: e.dma_start(out=tblA.rearrange("p a b c -> p (a b c)"), in_=tbl_d),
                      writes=["tblA"] + dead, dma="tblA")
            if PREFETCH and c == 14:
                sc.op("pool", lambda e: e.dma_start(out=Wout, in_=wout_d.rearrange("(kc p) n -> p kc n", p=128)),
                      writes=["Wout"] + dead, dma="Wout")
            if stop is not None and stop.startswith("AC") and c == int(stop[2:]):
                while pending:
                    pending.pop(0)()
                dump(arena[:, 0:53248], 53248)
            if c % 2 == 1 or c == N_FM - 1:
                nb_ = c // 2 + 2
                if nb_ < len(wblocks):
                    load_wblock(nb_)
        while pending:
            pending.pop(0)()
        while v_next[0] < 16:
            emit_v_job()

        sc.barrier()
        if stop == "A":
            dump(arena[:, 0:53248], 53248)

        if not PREFETCH:
            sc.op("pool", lambda e: e.dma_start(out=tblA.rearrange("p a b c -> p (a b c)"), in_=tbl_d), writes=["tblA"], dma="tblA")
            sc.op("pool", lambda e: e.dma_start(out=Wout, in_=wout_d.rearrange("(kc p) n -> p kc n", p=128)), writes=["Wout"], dma="Wout")
            sc.op("sp", lambda e: e.dma_start(out=gbt, in_=gb_d), writes=["gbt"], dma="gbt")
        sc.op("sp", lambda e: e.dma_start(out=VAo[0:64, :, :], in_=VA[64:128, 0:15, :]), writes=["VAo"], dma="vao0")
        sc.op("sp", lambda e: e.dma_start(out=VAo[64:128, :, :], in_=VA[0:64, 1:16, :]), writes=["VAo"], dma="vao1")

        bias_rr = [0]
        if not BIAS_PE:
            tflat = tblA.rearrange("p a b c -> p (a b c)")
            for k in range(14):
                sc.op("act", lambda e, k=k: e.activation(out=tflat[:, k * 512:(k + 1) * 512], in_=tflat[:, k * 512:(k + 1) * 512], func=AF.Exp),
                      reads=["tblA"], writes=["tblA"])
        acc_ring = Ring([0, 2])
        scr_ring = Ring([4, 5, 6, 7])
        pt_ring = Ring([0, 1, 2, 3])
        nrm_ring = Ring([0, 1])
        pv_pending = []

        def flush_pv(keep=0):
            while len(pv_pending) > keep:
                pv_pending.pop(0)()

        def zero_acc(b):
            sc.op("dve", lambda e: e.memset(ps[:, b * 512:(b + 2) * 512], 0.0), writes=[("ps", b), ("ps", b + 1)])

        def na_unit(QB, hp, ms):
            ab = acc_ring.next()
            OA, DA = ab, ab + 1
            jobs = _na_jobs(QB)
            banks_jobs = []
            cur = []
            ncol = 0
            for a0, Rs in jobs:
                n = 64 * len(Rs)
                if ncol + n > 512:
                    banks_jobs.append(cur)
                    cur = []
                    ncol = 0
                cur.append((a0, Rs, ncol))
                ncol += n
            if cur:
                banks_jobs.append(cur)
            for bj in banks_jobs:
                sbs = [scr_ring.next(), scr_ring.next()]
                pts = [pt_ring.next(), pt_ring.next()]
                tot = bj[-1][2] + 64 * len(bj[-1][1])
                g_bias, g_s = [[], []], [[], []]
                for hh in range(2):
                    h = 2 * hp + hh
                    pbase = 64 * hh
                    sb = sbs[hh]
                    for a0, Rs, co in bj:
                        rc = co
                        for (R0, st, n) in _pruns(Rs):
                            o3 = ps[:, sb * 512 + rc:sb * 512 + rc + 64 * n].rearrange("p (n c) -> p n c", c=64)
                            dri0 = 6 - (a0 - R0)
                            brhs = tblA[:, h, dri0:dri0 + st * (n - 1) + 1:st, :]
                            qv = QA_T[pbase:pbase + 64, hp, :].rearrange("p (r c) -> p r c", c=64)[:, R0:R0 + st * (n - 1) + 1:st, :]
                            g_bias[hh].append((o3, brhs))
                            g_s[hh].append((o3, qv, KA_T[pbase:pbase + 64, hp, a0 * 64:a0 * 64 + 128]))
                            rc += 64 * n
                for hh in range(2):
                    for gi, (o3, brhs) in enumerate(g_bias[hh]):
                        sc.op("pe", lambda e, o3=o3, brhs=brhs, gi=gi: e.matmul(
                            o3, lhsT=ident_bf, rhs=brhs, start=(gi == 0), stop=False, skip_group_check=True),
                            reads=["tblA", "cbf"], writes=[("ps", sbs[hh])])
                for gi in range(len(g_s[0])):
                    for hh in range(2):
                        (o3, qv, kl) = g_s[hh][gi]
                        sc.op("pe", lambda e, o3=o3, qv=qv, kl=kl, tp_=(64 * hh, 0): e.matmul(
                            o3, lhsT=kl, rhs=qv, start=False, stop=True, tile_position=tp_, skip_group_check=True),
                            reads=[], writes=[("ps", sbs[hh])])
                flush_pv(2)
                for hh in range(2):
                    sc.op("act", lambda e, sb=sbs[hh], pt=pts[hh], tot=tot: e.activation(out=PT[pt][:, 0:tot], in_=bank(sb, tot), func=AF.Exp),
                          reads=[("ps", sbs[hh])], writes=[("PT", pts[hh])])
                for hh in range(2):
                    def pv(bj=bj, pt=pts[hh], h=2 * hp + hh, pbase=64 * hh, OA=OA, DA=DA):
                        for a0, Rs, co in bj:
                            rc = co
                            vl = VA[:, a0 // 2, h * 64:(h + 1) * 64] if a0 % 2 == 0 else VAo[:, (a0 - 1) // 2, h * 64:(h + 1) * 64]
                            for (R0, st, n) in _pruns(Rs):
                                Rl = R0 - 8 * QB
                                pos0 = (R0 % 2) * 4 + Rl // 2
                                oa = ps[pbase:pbase + 64, OA * 512 + pos0 * 64:OA * 512 + (pos0 + n) * 64]
                                da = ps[pbase:pbase + 64, DA * 512 + pos0 * 64:DA * 512 + (pos0 + n) * 64]
                                p3 = PT[pt][:, rc:rc + 64 * n]
                                for (o_, l_, bk) in ((oa, vl, OA), (da, ones_bf[:, 0:64], DA)):
                                    sc.op("pe", lambda e, o_=o_, l_=l_, p3=p3, tp_=(0, pbase): e.matmul(
                                        o_, lhsT=l_, rhs=p3, start=False, stop=False, skip_group_check=True, tile_position=tp_),
                                        reads=[("PT", pt), "ones", "VAo"], writes=[("ps", bk)])
                                rc += 64 * n
                    pv_pending.append(pv)
            flush_pv()
            nr = nrm_ring.next()
            qsl = slice(QB * 512, (QB + 1) * 512)
            if NA_RECIP_DVE:
                sc.op("dve", lambda e: e.reciprocal(out=rdenb[nr], in_=bank(DA)), reads=[("ps", DA)], writes=[("rden", nr)])
            else:
                sc.op("act", lambda e: e.activation(out=rdenb[nr], in_=bank(DA), func=AF.Ln), reads=[("ps", DA)], writes=[("rden", nr)])
                sc.op("act", lambda e: e.activation(out=rdenb[nr], in_=rdenb[nr], func=AF.Exp, scale=-1.0),
                      reads=[("rden", nr)], writes=[("rden", nr)])
            sc.op("dve", lambda e: e.tensor_tensor(out=g2b[nr].rearrange("p (i par c) -> p par i c", par=2, c=64),
                                                   in0=bank(OA).rearrange("p (par i c) -> p par i c", par=2, c=64),
                                                   in1=rdenb[nr].rearrange("p (par i c) -> p par i c", par=2, c=64), op=ALU.mult),
                  reads=[("ps", OA), ("rden", nr)], writes=[("g2", nr)])
            sc.op("pool", lambda e: e.tensor_tensor(out=mixT[ms][:, hp, :], in0=g2b[nr], in1=gateA[:, hp, qsl], op=ALU.mult),
                  reads=[("g2", nr)], writes=[("mix", ms, hp)])
            zero_acc(ab)

        def swa_unit(n, ms):
            ab = acc_ring.next()
            OB, DB = ab, ab + 1
            nl = n % 4
            for kb in (n - 1, n, n + 1):
                if kb < 0 or kb > 15:
                    continue
                sbs = [scr_ring.next(), scr_ring.next()]
                pts = [pt_ring.next(), pt_ring.next()]
                if kb != n:
                    kind = 0 if kb < n else 1
                    for g in range(2):
                        sc.op("pe", lambda e, sb=sbs[g], kind=kind: e.matmul(bank(sb), lhsT=ident_bf, rhs=masks_bf[:, kind, :], start=True, stop=False),
                              reads=["cbf"], writes=[("ps", sbs[g])])
                for j in range(4):
                    for g in range(2):
                        pb = 64 * g
                        st_ = (kb == n)
                        sp_ = (kb == n) or (j == 3)
                        sc.op("pe", lambda e, sb=sbs[g], j=j, kb=kb, pb=pb, st_=st_, sp_=sp_: e.matmul(
                            ps[:, sb * 512 + j * 128:sb * 512 + (j + 1) * 128], lhsT=KB_T[pb:pb + 64, kb * 128:(kb + 1) * 128],
                            rhs=QB_T[pb:pb + 64, j, n * 128:(n + 1) * 128], start=st_, stop=sp_, tile_position=(pb, 0)),
                            reads=[], writes=[("ps", sbs[g])])
                flush_pv(2)
                for g in range(2):
                    sc.op("act", lambda e, sb=sbs[g], pt=pts[g]: e.activation(out=PT[pt], in_=bank(sb), func=AF.Exp, scale=0.125),
                          reads=[("ps", sbs[g])], writes=[("PT", pts[g])])
                for g in range(2):
                    def pv(pt=pts[g], kb=kb, pb=64 * g, OB=OB, DB=DB):
                        sc.op("pe", lambda e: e.matmul(ps[pb:pb + 64, OB * 512:OB * 512 + 512], lhsT=VB[:, kb, pb:pb + 64], rhs=PT[pt],
                                                       start=False, stop=False, skip_group_check=True, tile_position=(0, pb)),
                              reads=[("PT", pt)], writes=[("ps", OB)])
                        sc.op("pe", lambda e: e.matmul(ps[pb:pb + 64, DB * 512:DB * 512 + 512], lhsT=ones_bf[:, 0:64], rhs=PT[pt],
                                                       start=False, stop=False, skip_group_check=True, tile_position=(0, pb)),
                              reads=[("PT", pt)], writes=[("ps", DB)])
                    pv_pending.append(pv)
            flush_pv()
            nr = nrm_ring.next()
            d3 = rdenb[nr].rearrange("p (j q) -> p j q", j=4)
            sc.op("dve", lambda e: e.tensor_tensor(out=d3, in0=bank(DB).rearrange("p (j q) -> p j q", j=4),
                                                   in1=sinkE.unsqueeze(2).broadcast_to([128, 4, 128]), op=ALU.add),
                  reads=[("ps", DB), "sinkE"], writes=[("rden", nr)])
            if n % 4 < 2:
                sc.op("dve", lambda e: e.reciprocal(out=rdenb[nr], in_=rdenb[nr]), reads=[("rden", nr)], writes=[("rden", nr)])
            else:
                sc.op("act", lambda e: e.activation(out=rdenb[nr], in_=rdenb[nr], func=AF.Ln), reads=[("rden", nr)], writes=[("rden", nr)])
                sc.op("act", lambda e: e.activation(out=rdenb[nr], in_=rdenb[nr], func=AF.Exp, scale=-1.0),
                      reads=[("rden", nr)], writes=[("rden", nr)])
            sc.op("dve", lambda e: e.tensor_tensor(out=g2b[nr], in0=bank(OB), in1=rdenb[nr], op=ALU.mult),
                  reads=[("ps", OB), ("rden", nr)], writes=[("g2", nr)])
            sc.op("pool", lambda e: e.tensor_tensor(out=mixT[ms][:, 4:8, nl * 128:(nl + 1) * 128],
                                                    in0=g2b[nr].rearrange("p (j q) -> p j q", j=4),
                                                    in1=gateB[:, :, n * 128:(n + 1) * 128], op=ALU.mult),
                  reads=[("g2", nr)], writes=[("mix", ms, 4 + nl)])
            zero_acc(ab)

        def load_xres(i):
            sl = i % 4
            sc.op("sp", lambda e: e.dma_start(out=xf[sl], in_=x_d[i * 128:(i + 1) * 128, :]), writes=[("xf", sl)], dma=("xf", sl))

        out_back = []

        def out_unit(i, ms):
            il = i % 4
            sl = i % 4
            b = scr_ring.next_pair()
            for nb in range(2):
                for c in range(8):
                    sc.op("pe", lambda e, nb=nb, c=c: e.matmul(bank(b + nb), lhsT=mixT[ms][:, c, il * 128:(il + 1) * 128],
                                                              rhs=Wout[:, c, nb * 512:(nb + 1) * 512], start=(c == 0), stop=(c == 7)),
                          reads=[("mix", ms, cc) for cc in range(4)] + [("mix", ms, 4 + il), "Wout"],
                          writes=[("ps", b + nb)])
            yps = ps[:, b * 512:b * 512 + 1024]
            sc.op("dve", lambda e: e.scalar_tensor_tensor(out=xf[sl], in0=xf[sl], scalar=ALPHA, in1=yps, op0=ALU.mult, op1=ALU.add),
                  reads=[("ps", b), ("ps", b + 1), ("xf", sl)], writes=[("xf", sl)])
            sc.op("dve", lambda e: e.bn_stats(out=stt[sl][:, 0:6], in_=xf[sl][:, 0:512]), reads=[("xf", sl)], writes=[("st", sl, 0)])
            sc.op("dve", lambda e: e.bn_stats(out=stt[sl][:, 6:12], in_=xf[sl][:, 512:1024]), reads=[("xf", sl)], writes=[("st", sl, 1)])
            sc.op("dve", lambda e: e.bn_aggr(out=mvt[sl], in_=stt[sl]), reads=[("st", sl, 0), ("st", sl, 1)], writes=[("mv", sl)])
            sc.op("dve", lambda e: e.tensor_scalar_add(out=rst[sl][:, 1:2], in0=mvt[sl][:, 1:2], scalar1=LN_EPS),
                  reads=[("mv", sl)], writes=[("rs", sl, 1)])
            sc.op("act", lambda e: e.activation(out=rst[sl][:, 1:2], in_=rst[sl][:, 1:2], func=AF.Ln),
                  reads=[("rs", sl, 1)], writes=[("rs", sl, 1)])
            sc.op("act", lambda e: e.activation(out=rst[sl][:, 0:1], in_=rst[sl][:, 1:2], func=AF.Exp, scale=-0.5),
                  reads=[("rs", sl, 1)], writes=[("rs", sl, 0)])

            def back():
                if i >= 15:
                    sc.op("dve", lambda e: e.scalar_tensor_tensor(out=xf[sl], in0=xf[sl], scalar=mvt[sl][:, 0:1], in1=gbt[:, 0:D],
                                                                  op0=ALU.subtract, op1=ALU.mult),
                          reads=[("xf", sl), ("mv", sl), "gbt"], writes=[("xf", sl)])
                    sc.op("dve", lambda e: e.scalar_tensor_tensor(out=xf[sl], in0=xf[sl], scalar=rst[sl][:, 0:1], in1=gbt[:, D:2 * D],
                                                                  op0=ALU.mult, op1=ALU.add),
                          reads=[("xf", sl), ("rs", sl, 0), "gbt"], writes=[("xf", sl)])
                else:
                    sc.op("dve", lambda e: e.tensor_scalar(out=xf[sl], in0=xf[sl], scalar1=mvt[sl][:, 0:1], scalar2=rst[sl][:, 0:1],
                                                           op0=ALU.subtract, op1=ALU.mult),
                          reads=[("xf", sl), ("mv", sl), ("rs", sl, 0)], writes=[("xf", sl)])
                    sc.op("dve" if i >= 12 else "pool", lambda e: e.tensor_tensor(out=xf[sl], in0=xf[sl], in1=gbt[:, 0:D], op=ALU.mult),
                          reads=[("xf", sl), "gbt"], writes=[("xf", sl)])
                    sc.op("pool", lambda e: e.tensor_tensor(out=xf[sl], in0=xf[sl], in1=gbt[:, D:2 * D], op=ALU.add),
                          reads=[("xf", sl), "gbt"], writes=[("xf", sl)])
                sc.op("sp", lambda e: e.dma_start(out=out_d[i * 128:(i + 1) * 128, :], in_=xf[sl]),
                      reads=[("xf", sl)], dma=("xo", sl))
            while out_back:
                out_back.pop(0)()
            out_back.append(back)

        zero_acc(0)
        zero_acc(2)
        for QB in range(4):
            ms = QB % 2
            if QB == 0:
                for n in range(4):
                    swa_unit(n, ms)
                for i in range(4):
                    load_xres(i)
                for hp in range(4):
                    na_unit(QB, hp, ms)
                if stop == "B0":
                    dump(mixT[0].rearrange("p a b -> p (a b)"), 4096)
                for i in range(4):
                    out_unit(i, ms)
            else:
                for i in range(4 * QB, 4 * QB + 4):
                    load_xres(i)
                for k in range(4):
                    na_unit(QB, k, ms)
                    swa_unit(4 * QB + k, ms)
                for n in range(4 * QB, 4 * QB + 4):
                    out_unit(n, ms)
            while out_back:
                out_back.pop(0)()
            if stop is not None and stop.startswith("CQ") and QB == int(stop[2:]):
                dump(mixT[0].rearrange("p a b -> p (a b)"), 4096)

    try:
        body()
    except _Stop:
        pass

    fin = [(sk, v) for sk, v in sc.dma_cnt.items() if isinstance(sk[1], tuple) and sk[1][0] == "xo"]
    sc.streams["sp"].append((fin, None, None))

    with ExitStack() as es:
        sems = {}
        for e in ENGS:
            sems[e] = es.enter_context(nc.semaphore("s_" + e))
        for k, sk in enumerate(sc.dma_cnt.keys()):
            sems[sk] = es.enter_context(nc.semaphore("d_%d" % k))
        block = es.enter_context(nc.Block())

        def emit(eng_handle, name):
            for waits, fn, inc in sc.streams[name]:
                for sk, v in waits:
                    eng_handle.wait_ge(sems[sk], v)
                if fn is not None:
                    ins = fn(eng_handle)
                    ins.then_inc(sems[inc[0]], inc[1])

        @block.tensor
        def _(t):
            emit(t, "pe")

        @block.scalar
        def _(a):
            emit(a, "act")

        @block.vector
        def _(v):
            emit(v, "dve")

        @block.gpsimd
        def _(g):
            emit(g, "pool")

        @block.sync
        def _(s):
            emit(s, "sp")
    return nc


_NC_CACHE = {}


def kernel(x, w_in, rel_pos_bias, sink_logits, w_out, ln_gain, ln_bias):
    x = np.asarray(x, dtype=np.float32)
    w_in = np.asarray(w_in, dtype=np.float32)[0]
    w_out = np.asarray(w_out, dtype=np.float32)[0]
    rpb = np.asarray(rel_pos_bias, dtype=np.float32)[0]
    sink = np.asarray(sink_logits, dtype=np.float32)[0]
    gain = np.asarray(ln_gain, dtype=np.float32)[0]
    bias = np.asarray(ln_bias, dtype=np.float32)[0]

    w_in_p = np.ascontiguousarray(w_in[:, _w_in_perm()])
    w_out_p = np.ascontiguousarray(w_out[_w_out_perm(), :])
    tbl_a = _na_table(rpb)
    cmat, cosT, sinT = _consts()
    sink_lay = np.ascontiguousarray(np.repeat(sink.reshape(2, 4), 64, axis=0))
    gb = np.ascontiguousarray(np.tile(np.concatenate([gain, bias])[None, :], (128, 1)))

    if "nc" not in _NC_CACHE:
        _NC_CACHE["nc"] = build_nc()
    nc = _NC_CACHE["nc"]
    shared = {"w_in_p": w_in_p, "w_out_p": w_out_p, "tbl_a": tbl_a, "cmat": cmat, "cos_t": cosT, "sin_t": sinT,
              "sink_lay": sink_lay, "gain_bias": gb}
    in_maps = []
    for b in range(N_CORES):
        m = dict(shared)
        m["x"] = np.ascontiguousarray(x[b])
        in_maps.append(m)
    res = run_bass_kernel_spmd(nc, in_maps, core_ids=list(range(N_CORES)))
    out = np.stack([np.asarray(r["out"], dtype=np.float32).reshape(S, D) for r in res.results], axis=0)
    return out
```
